# Optimizing a Trainium2 kernel written in Bass

```python
import math
import jax, jax.numpy as jnp
from jax import lax
import numpy as np

D_MODEL = 2048
BATCH = 4
SEQ = 4096
DEPTH = 1

GRID_W = 64
CTX_LEN = 256
N_DIFF_HEADS = 8
DIFF_HEAD_DIM = 64
DIFF_V_DIM = 2 * DIFF_HEAD_DIM
ATTN_WIDTH = N_DIFF_HEADS * DIFF_V_DIM
CONV_WIDTH = D_MODEL - ATTN_WIDTH
CONV_GROUPS = 8
CONV_KERNEL = 31
IN_COLS = 3 * ATTN_WIDTH + 2 * CONV_WIDTH
D_FF = 4 * D_MODEL
ROPE_BASE = 10000.0
Q_BLOCK = 128
LN_EPS = 1e-5
DEEPNORM_ALPHA = (2.0 * DEPTH) ** 0.25
DEEPNORM_BETA = (8.0 * DEPTH) ** -0.25

kernel_name = "hymba_style_diffattn_conformer_dit_block"


def layer_norm(x, g=None, b=None):
    xf = x.astype(jnp.float32)
    mu = jnp.mean(xf, axis=-1, keepdims=True)
    var = jnp.mean(jnp.square(xf - mu), axis=-1, keepdims=True)
    y = (xf - mu) * lax.rsqrt(var + LN_EPS)
    if g is not None:
        y = y * g.astype(jnp.float32) + b.astype(jnp.float32)
    return y.astype(x.dtype)


def rms_norm(x, g):
    xf = x.astype(jnp.float32)
    y = xf * lax.rsqrt(jnp.mean(jnp.square(xf), axis=-1, keepdims=True) + LN_EPS)
    return (y * g.astype(jnp.float32)).astype(x.dtype)


def modulate(x, shift, scale):
    return layer_norm(x) * (1 + scale) + shift


def axial_rope_tables(n_tokens):
    rows = n_tokens // GRID_W
    row = jnp.repeat(jnp.arange(rows), GRID_W, total_repeat_length=n_tokens).astype(jnp.float32)
    col = jnp.tile(jnp.arange(GRID_W), rows).astype(jnp.float32)
    axis_dim = DIFF_HEAD_DIM // 2
    inv_freq = ROPE_BASE ** (-jnp.arange(0, axis_dim, 2, dtype=jnp.float32) / axis_dim)
    ang_r = row[:, None] * inv_freq
    ang_c = col[:, None] * inv_freq
    ang = jnp.concatenate([ang_r, ang_r, ang_c, ang_c], axis=-1)
    return jnp.cos(ang), jnp.sin(ang)


def _rotate_half(u):
    u1, u2 = jnp.split(u, 2, axis=-1)
    return jnp.concatenate([-u2, u1], axis=-1)


def apply_axial_rope(x, cos, sin):
    xr, xc = jnp.split(x, 2, axis=-1)
    rotated = jnp.concatenate([_rotate_half(xr), _rotate_half(xc)], axis=-1)
    cos = cos[:, None, None, :].astype(x.dtype)
    sin = sin[:, None, None, :].astype(x.dtype)
    return x * cos + rotated * sin


def diff_attention(q, k, v, lam, lam_init, subln_g):
    B, Lq = q.shape[0], q.shape[1]
    nb = Lq // Q_BLOCK
    qb = q.reshape(B, nb, Q_BLOCK, N_DIFF_HEADS, 2, DIFF_HEAD_DIM).transpose(1, 0, 2, 3, 4, 5)
    scale = DIFF_HEAD_DIM ** -0.5

    def block(qi):
        s = jnp.einsum('bqhcd,bkhcd->bhcqk', qi, k, preferred_element_type=jnp.float32) * scale
        p = jax.nn.softmax(s, axis=-1)
        a = p[:, :, 0] - lam * p[:, :, 1]
        return jnp.einsum('bhqk,bkhe->bqhe', a.astype(v.dtype), v)

    o = lax.map(block, qb)
    o = o.transpose(1, 0, 2, 3, 4).reshape(B, Lq, N_DIFF_HEADS, DIFF_V_DIM)
    o = rms_norm(o, subln_g) * (1.0 - lam_init)
    return o.reshape(B, Lq, ATTN_WIDTH)


def conformer_conv(u, w_dw, b_dw, g, b):
    a, gate = jnp.split(u, 2, axis=-1)
    h = a * jax.nn.sigmoid(gate)
    h = lax.conv_general_dilated(
        h, w_dw[:, None, :].astype(h.dtype), window_strides=(1,),
        padding=[(CONV_KERNEL // 2, CONV_KERNEL // 2)],
        dimension_numbers=('NWC', 'WIO', 'NWC'),
        feature_group_count=CONV_WIDTH) + b_dw
    return jax.nn.silu(layer_norm(h, g, b))


def sq_relu_mlp(h, w1, b1, w2, b2):
    return jnp.square(jax.nn.relu(h @ w1 + b1)) @ w2 + b2


def hybrid_layer(x, ctx, c, c_ctx, cos, sin, lam_init, update_ctx,
                 w_ada, b_ada, w_in, b_glu, lq1, lk1, lq2, lk2, subln_g,
                 w_dw, b_dw, conv_ln_g, conv_ln_b, w_out, b_out, ln1_g, ln1_b,
                 w_ff1, b_ff1, w_ff2, b_ff2, ln2_g, ln2_b):
    B, L, _ = x.shape
    C = ctx.shape[1]
    A = ATTN_WIDTH
    ada = (jax.nn.silu(c) @ w_ada + b_ada)[:, None, :]
    sh1, sc1, g1, sh2, sc2, g2 = jnp.split(ada, 6, axis=-1)
    ada_c = jax.nn.silu(c_ctx) @ w_ada + b_ada
    csh1, csc1, cg1, csh2, csc2, cg2 = jnp.split(ada_c, 6, axis=-1)

    lam = (jnp.exp(jnp.sum(lq1.astype(jnp.float32) * lk1.astype(jnp.float32)))
           - jnp.exp(jnp.sum(lq2.astype(jnp.float32) * lk2.astype(jnp.float32))) + lam_init)

    h = modulate(x, sh1, sc1)
    hc = modulate(ctx, csh1, csc1)
    p = h @ w_in
    q, k, v, u = jnp.split(p, [A, 2 * A, 3 * A], axis=-1)
    q = apply_axial_rope(q.reshape(B, L, N_DIFF_HEADS, 2, DIFF_HEAD_DIM), cos, sin)
    k = apply_axial_rope(k.reshape(B, L, N_DIFF_HEADS, 2, DIFF_HEAD_DIM), cos, sin)
    v = v.reshape(B, L, N_DIFF_HEADS, DIFF_V_DIM)
    kvc = hc @ w_in[:, A:3 * A]
    kc, vc = jnp.split(kvc, 2, axis=-1)
    kc = kc.reshape(B, C, N_DIFF_HEADS, 2, DIFF_HEAD_DIM)
    vc = vc.reshape(B, C, N_DIFF_HEADS, DIFF_V_DIM)

    k_all = jnp.concatenate([k, kc], axis=1)
    v_all = jnp.concatenate([v, vc], axis=1)
    attn = diff_attention(q, k_all, v_all, lam, lam_init, subln_g)
    conv = conformer_conv(u + b_glu, w_dw, b_dw, conv_ln_g, conv_ln_b)
    y = jnp.concatenate([attn, conv], axis=-1) @ w_out + b_out
    x = layer_norm(DEEPNORM_ALPHA * x + g1 * y, ln1_g, ln1_b)

    h2 = modulate(x, sh2, sc2)
    x = layer_norm(DEEPNORM_ALPHA * x + g2 * sq_relu_mlp(h2, w_ff1, b_ff1, w_ff2, b_ff2), ln2_g, ln2_b)

    if update_ctx:
        qc = (hc @ w_in[:, :A]).reshape(B, C, N_DIFF_HEADS, 2, DIFF_HEAD_DIM)
        uc = hc @ w_in[:, 3 * A:] + b_glu
        attn_c = diff_attention(qc, kc, vc, lam, lam_init, subln_g)
        conv_c = conformer_conv(uc, w_dw, b_dw, conv_ln_g, conv_ln_b)
        yc = jnp.concatenate([attn_c, conv_c], axis=-1) @ w_out + b_out
        ctx = layer_norm(DEEPNORM_ALPHA * ctx + cg1 * yc, ln1_g, ln1_b)
        hc2 = modulate(ctx, csh2, csc2)
        ctx = layer_norm(DEEPNORM_ALPHA * ctx + cg2 * sq_relu_mlp(hc2, w_ff1, b_ff1, w_ff2, b_ff2),
                         ln2_g, ln2_b)
    return x, ctx


def setup_inputs(seed: int = 0) -> dict:
    key = jax.random.key(seed)
    ks = jax.random.split(key, 32)
    f32 = jnp.float32
    D = D_MODEL

    def nrm(k, shape, std):
        return jax.random.normal(k, shape, f32) * std

    return {
        "x": nrm(ks[0], (BATCH, SEQ, D), 1.0),
        "c": nrm(ks[1], (BATCH, D), 1.0),
        "ctx": nrm(ks[2], (BATCH, CTX_LEN, D), 1.0),
        "c_ctx": nrm(ks[3], (D,), 1.0),
        "w_ada": nrm(ks[4], (DEPTH, D, 6 * D), D ** -0.5),
        "b_ada": nrm(ks[5], (DEPTH, 6 * D), 0.01),
        "w_in": nrm(ks[6], (DEPTH, D, IN_COLS), D ** -0.5),
        "b_glu": nrm(ks[7], (DEPTH, 2 * CONV_WIDTH), 0.01),
        "lambda_q1": nrm(ks[8], (DEPTH, DIFF_HEAD_DIM), 0.1),
        "lambda_k1": nrm(ks[9], (DEPTH, DIFF_HEAD_DIM), 0.1),
        "lambda_q2": nrm(ks[10], (DEPTH, DIFF_HEAD_DIM), 0.1),
        "lambda_k2": nrm(ks[11], (DEPTH, DIFF_HEAD_DIM), 0.1),
        "subln_g": 1.0 + nrm(ks[12], (DEPTH, DIFF_V_DIM), 0.01),
        "w_dw": nrm(ks[13], (DEPTH, CONV_KERNEL, CONV_WIDTH), CONV_KERNEL ** -0.5),
        "b_dw": nrm(ks[14], (DEPTH, CONV_WIDTH), 0.01),
        "conv_ln_g": 1.0 + nrm(ks[15], (DEPTH, CONV_WIDTH), 0.01),
        "conv_ln_b": nrm(ks[16], (DEPTH, CONV_WIDTH), 0.01),
        "w_out": nrm(ks[17], (DEPTH, D, D), D ** -0.5 * DEEPNORM_BETA),
        "b_out": nrm(ks[18], (DEPTH, D), 0.01),
        "ln1_g": 1.0 + nrm(ks[19], (DEPTH, D), 0.01),
        "ln1_b": nrm(ks[20], (DEPTH, D), 0.01),
        "w_ff1": nrm(ks[21], (DEPTH, D, D_FF), D ** -0.5),
        "b_ff1": nrm(ks[22], (DEPTH, D_FF), 0.01),
        "w_ff2": nrm(ks[23], (DEPTH, D_FF, D), D_FF ** -0.5 * DEEPNORM_BETA),
        "b_ff2": nrm(ks[24], (DEPTH, D), 0.01),
        "ln2_g": 1.0 + nrm(ks[25], (DEPTH, D), 0.01),
        "ln2_b": nrm(ks[26], (DEPTH, D), 0.01),
    }


def reference(x, c, ctx, c_ctx, w_ada, b_ada, w_in, b_glu, lambda_q1, lambda_k1,
              lambda_q2, lambda_k2, subln_g, w_dw, b_dw, conv_ln_g, conv_ln_b,
              w_out, b_out, ln1_g, ln1_b, w_ff1, b_ff1, w_ff2, b_ff2, ln2_g, ln2_b):
    cos, sin = axial_rope_tables(x.shape[1])
    for l in range(DEPTH):
        lam_init = 0.8 - 0.6 * math.exp(-0.3 * l)
        x, ctx = hybrid_layer(
            x, ctx, c, c_ctx, cos, sin, lam_init, l < DEPTH - 1,
            w_ada[l], b_ada[l], w_in[l], b_glu[l], lambda_q1[l], lambda_k1[l],
            lambda_q2[l], lambda_k2[l], subln_g[l], w_dw[l], b_dw[l], conv_ln_g[l],
            conv_ln_b[l], w_out[l], b_out[l], ln1_g[l], ln1_b[l], w_ff1[l], b_ff1[l],
            w_ff2[l], b_ff2[l], ln2_g[l], ln2_b[l])
    return x
```

```python
import contextlib
import math
import numpy as np
import concourse.bass as bass
import concourse.mybir as mybir
from concourse.bass_utils import run_bass_kernel_spmd

F32 = mybir.dt.float32
BF16 = mybir.dt.bfloat16
AF = mybir.ActivationFunctionType
ALU = mybir.AluOpType
ENGS = ["sync", "scalar", "vector", "gpsimd", "tensor"]

D = 2048
L = 4096
T = 2048
CTX = 256
NKEY = 4352
EPS = 1e-5
ALPHA = 2.0 ** 0.25
LAM_INIT = 0.2


class Res:
    __slots__ = ("name", "last_w", "readers")

    def __init__(self, name):
        self.name = name
        self.last_w = None
        self.readers = []


class Op:
    __slots__ = ("eng", "fn", "deps", "dma", "signal", "event", "idx")


class Prog:
    def __init__(self, nc, n_dma_sems=16):
        self.nc = nc
        self.ops = []
        self.n_dma_sems = n_dma_sems
        self.last_op = {e: None for e in ENGS}
        self.pend = {e: [] for e in ENGS}
        self.dmas_since = []

    def add(self, eng, fn, reads=(), writes=(), dma=False, loose=False):
        op = Op()
        op.eng = eng
        op.fn = fn
        op.dma = dma
        op.signal = dma
        op.event = None
        op.idx = len(self.ops)
        deps = list(self.pend[eng])
        self.pend[eng] = []
        for r in reads:
            if r.last_w is not None:
                deps.append(r.last_w)
            if dma:
                r.readers.append(op)
            else:
                r.readers = [o for o in r.readers if o.dma or o.eng != eng]
                r.readers.append(op)
        for w in writes:
            if w.last_w is not None:
                deps.append(w.last_w)
            deps.extend(w.readers)
            w.last_w = op
            w.readers = []
        dd = []
        seen = set()
        for d in deps:
            if d is op or d.idx in seen:
                continue
            if eng == "tensor" and d.eng == "tensor" and not d.dma:
                continue
            if loose and d.eng == eng and not d.dma:
                continue
            seen.add(d.idx)
            dd.append(d)
            d.signal = True
        op.deps = dd
        self.ops.append(op)
        self.last_op[eng] = op
        if dma:
            self.dmas_since.append(op)
        return op

    def barrier(self):
        lst = [o for o in self.last_op.values() if o is not None] + self.dmas_since
        self.dmas_since = []
        for e in ENGS:
            self.pend[e] = list(lst)
        for o in lst:
            o.signal = True

    def emit(self, final_wait_eng="sync"):
        nc = self.nc
        with contextlib.ExitStack() as st:
            engsem = {e: st.enter_context(nc.semaphore("c_" + e)) for e in ENGS}
            dsem = {e: [st.enter_context(nc.semaphore("d_%s_%d" % (e, i))) for i in range(self.n_dma_sems)]
                    for e in ("sync", "scalar", "gpsimd")}
            cnt = {e: 0 for e in ENGS}
            dval = {e: [0] * self.n_dma_sems for e in dsem}
            dlast = {e: [None] * self.n_dma_sems for e in dsem}
            drr = {e: 0 for e in dsem}
            for op in self.ops:
                if op.dma:
                    e = op.eng
                    i = drr[e]
                    drr[e] = (i + 1) % self.n_dma_sems
                    prev = dlast[e][i]
                    if prev is not None:
                        op.deps.append(prev)
                    dval[e][i] += 16
                    op.event = (dsem[e][i], dval[e][i], ("d", e, i))
                    dlast[e][i] = op
                elif op.signal:
                    cnt[op.eng] += 1
                    op.event = (engsem[op.eng], cnt[op.eng], ("c", op.eng))
            tail = [op for op in self.ops if op.dma]
            streams = {e: [o for o in self.ops if o.eng == e] for e in ENGS}
            self.stats = {e: len(streams[e]) for e in ENGS}
            self.cnts = dict(cnt)
            block = st.enter_context(nc.Block())

            def run_stream(ename, eng):
                waited = {}
                for op in streams[ename]:
                    for d in op.deps:
                        sem, val, key = d.event
                        if waited.get(key, 0) < val:
                            eng.wait_ge(sem, val)
                            waited[key] = val
                    ins = op.fn(eng)
                    if op.event is not None:
                        sem, val, key = op.event
                        ins.then_inc(sem, 16 if op.dma else 1)
                if ename == final_wait_eng:
                    mx = {}
                    for op in tail:
                        sem, val, key = op.event
                        if key not in mx or mx[key][1] < val:
                            mx[key] = (sem, val)
                    for key, (sem, val) in mx.items():
                        if waited.get(key, 0) < val:
                            eng.wait_ge(sem, val)
                            waited[key] = val

            @block.sync
            def _(eng):
                run_stream("sync", eng)

            @block.scalar
            def _(eng):
                run_stream("scalar", eng)

            @block.vector
            def _(eng):
                run_stream("vector", eng)

            @block.gpsimd
            def _(eng):
                run_stream("gpsimd", eng)

            @block.tensor
            def _(eng):
                run_stream("tensor", eng)


class Arena:
    def __init__(self, nc, limit):
        self.nc = nc
        self.off = 16512
        self.limit = limit
        self.n = 0

    def alloc(self, name, shape, dt):
        esz = 2 if dt == BF16 else 4
        size = esz * int(np.prod(shape[1:]))
        size = (size + 31) // 32 * 32
        self.n += 1
        t = self.nc.alloc_sbuf_tensor_at("%s_%d" % (name, self.n), list(shape), dt, offset=self.off)
        self.off += size
        assert self.off <= self.limit, (name, self.off, self.limit)
        return t


class Pool:
    def __init__(self, items):
        self.items = items
        self.i = 0

    def next(self):
        it = self.items[self.i % len(self.items)]
        self.i += 1
        return it


def build_program(debug=False, stop_after="D"):
    nc = bass.Bass("TRN2", target_bir_lowering=False)
    P = Prog(nc)

    def din(name, shape):
        return nc.dram_tensor(name, list(shape), F32, kind="ExternalInput").ap()

    x_own = din("x_own", [T, D])
    x_oth = din("x_oth", [T, D])
    ctxb = din("ctxb", [CTX, D])
    cT2 = din("cT2", [128, 32])
    w_ada = din("w_ada", [D, 6 * D])
    w_in = din("w_in", [D, 5120])
    w_out = din("w_out", [D, D])
    w_ff1 = din("w_ff1", [D, 4 * D])
    w_ff2 = din("w_ff2", [4 * D, D])
    b_adaT = din("b_adaT", [128, 96])
    b_ada_g = din("b_ada_g", [2, D])
    b_gluT = din("b_gluT", [128, 16])
    lam4 = din("lam4", [4, 64])
    subg = din("subg", [1, 128])
    w_dwT = din("w_dwT", [128, 8 * 31])
    cvec = din("cvec", [128, 24])
    b_out = din("b_out", [1, D])
    ln1 = din("ln1", [2, D])
    b1T = din("b1T", [128, 64])
    b_ff2 = din("b_ff2", [1, D])
    ln2 = din("ln2", [2, D])
    rope = din("rope", [4, 128, T])
    hmask = din("hmask", [128, 2])
    consts = din("consts", [128, 384])
    out = nc.dram_tensor("out", [T, D], F32, kind="ExternalOutput").ap()
    skind = "ExternalOutput" if debug else "Internal"
    Qs = nc.dram_tensor("Qs", [8, 128, T], BF16, kind=skind).ap()
    Ks = nc.dram_tensor("Ks", [8, 128, NKEY], BF16, kind=skind).ap()
    Vs = nc.dram_tensor("Vs", [8, 128, 34, 128], BF16, kind=skind).ap()
    ACs = nc.dram_tensor("ACs", [16, 128, T], BF16, kind=skind).ap()

    A = Arena(nc, 229344)
    psm = nc.alloc_psum_tensor("psm", [128, 6, 512], F32)
    pst = nc.alloc_psum_tensor("pst", [128, 2, 1024], BF16)
    r_bank = [Res("bank%d" % i) for i in range(8)]

    def bank32(i):
        return psm[:, i, :] if i < 6 else pst[:, i - 6, :].bitcast(F32)

    def bank16(i):
        return psm[:, i, :].bitcast(BF16) if i < 6 else pst[:, i - 6, :]

    ident = A.alloc("ident", [128, 128], BF16)
    permm = A.alloc("permm", [128, 128], BF16)
    ones_bf = A.alloc("ones_bf", [128, 128], BF16)
    ones32 = A.alloc("ones32", [128, 128], F32)
    modv = A.alloc("modv", [128, 96, 2], F32)
    g_bc = [A.alloc("g1bc", [128, D], BF16), A.alloc("g2bc", [128, D], BF16)]
    bglu = A.alloc("bglu", [128, 16], F32)
    cv = A.alloc("cv", [128, 24], F32)
    b1s = A.alloc("b1s", [128, 64], F32)
    hm = A.alloc("hm", [128, 2], F32)
    wdw = A.alloc("wdw", [128, 8 * 31], F32)
    lamv = A.alloc("lamv", [128, 8], F32)
    gsub = A.alloc("gsub", [128, 128], F32)
    wbufs = [A.alloc("wblk", [128, 16, 256], BF16) for _ in range(3)]
    wpool = Pool([(wbufs[i], [Res("w%da" % i), Res("w%db" % i)]) for i in range(3)])
    c2 = A.alloc("c2", [128, 32], F32)
    cs = A.alloc("cs", [128, 16, 2], BF16)
    csb = A.alloc("csb", [128, 16, 128], BF16)
    badaT = A.alloc("badaT", [128, 96], F32)
    acc_bank = Pool([0, 1, 2, 3])
    r_cst = Res("cst")
    r_mod = Res("mod")
    r_g = [Res("g1"), Res("g2")]
    r_lam = Res("lam")
    persist_end = A.off

    def wview(W):
        return W.rearrange("(kc p) n -> p kc n", p=128)

    def load_w(Wv, kc0, n0, halves=None):
        buf, rr = wpool.next()
        if halves is None:
            P.add("gpsimd", lambda e: e.dma_start(out=buf[:], in_=Wv[:, kc0:kc0 + 16, n0:n0 + 256]),
                  writes=rr, dma=True)
        else:
            for hh, nn in enumerate(halves):
                P.add("gpsimd", lambda e, hh=hh, nn=nn: e.dma_start(
                    out=buf[:, :, hh * 128:(hh + 1) * 128], in_=Wv[:, kc0:kc0 + 16, nn:nn + 128]),
                    writes=[rr[hh]], dma=True)
        return buf, rr

    def stream(specs, fn, depth=2):
        loaded = []
        n = len(specs)
        for i in range(min(depth, n)):
            loaded.append(load_w(*specs[i][1]))
        for i in range(n):
            buf, rr = loaded[i]
            fn(specs[i][0], buf, rr)
            if i + depth < n:
                loaded.append(load_w(*specs[i + depth][1]))

    ph_mark = A.off
    cst32 = A.alloc("cst32", [128, 384], F32)
    r_c32 = Res("c32")
    P.add("sync", lambda e: e.dma_start(out=cst32[:], in_=consts), writes=[r_c32], dma=True)
    P.add("vector", lambda e: e.tensor_copy(out=ident[:], in_=cst32[:, 0:128]), reads=[r_c32], writes=[r_cst])
    P.add("vector", lambda e: e.tensor_copy(out=permm[:], in_=cst32[:, 128:256]), reads=[r_c32], writes=[r_cst])
    P.add("vector", lambda e: e.tensor_copy(out=ones_bf[:], in_=cst32[:, 256:384]), reads=[r_c32], writes=[r_cst])
    P.add("vector", lambda e: e.tensor_copy(out=ones32[:], in_=cst32[:, 256:384]), reads=[r_c32], writes=[r_cst])
    for dst, src in ((bglu, b_gluT), (cv, cvec), (b1s, b1T), (hm, hmask), (wdw, w_dwT)):
        P.add("sync", lambda e, dst=dst, src=src: e.dma_start(out=dst[:], in_=src), writes=[r_cst], dma=True)
    P.add("sync", lambda e: e.dma_start(out=gsub[:], in_=subg.partition_broadcast(128).rearrange("p a b -> p (a b)")),
          writes=[r_cst], dma=True)
    lam_in = A.alloc("lam_in", [128, 4, 64], F32)
    lam_t = A.alloc("lam_t", [128, 2, 64], F32)
    P.add("sync", lambda e: e.dma_start(out=lam_in[:].rearrange("p a b -> p (a b)"),
                                         in_=lam4.rearrange("a b -> (a b)").partition_broadcast(128)),
          writes=[r_lam], dma=True)
    P.add("vector", lambda e: e.tensor_tensor(out=lam_t[:, 0, :], in0=lam_in[:, 0, :], in1=lam_in[:, 1, :], op=ALU.mult),
          writes=[r_lam])
    P.add("vector", lambda e: e.tensor_tensor(out=lam_t[:, 1, :], in0=lam_in[:, 2, :], in1=lam_in[:, 3, :], op=ALU.mult),
          writes=[r_lam])
    P.add("vector", lambda e: e.reduce_sum(out=lamv[:, 0:2], in_=lam_t[:], axis=mybir.AxisListType.X), writes=[r_lam])
    P.add("scalar", lambda e: e.activation(out=lamv[:, 3:5], in_=lamv[:, 0:2], func=AF.Exp), writes=[r_lam])
    P.add("vector", lambda e: e.scalar_tensor_tensor(out=lamv[:, 2:3], in0=lamv[:, 4:5], scalar=-LAM_INIT,
                                                     in1=lamv[:, 3:4], op0=ALU.add, op1=ALU.subtract), writes=[r_lam])
    P.add("vector", lambda e: e.tensor_scalar(out=gsub[:], in0=gsub[:], scalar1=1.0 - LAM_INIT, scalar2=None,
                                              op0=ALU.mult), writes=[r_cst])

    r_c2 = Res("c2")
    P.add("sync", lambda e: e.dma_start(out=c2[:], in_=cT2), writes=[r_c2], dma=True)
    P.add("sync", lambda e: e.dma_start(out=badaT[:], in_=b_adaT), writes=[r_c2], dma=True)
    P.add("scalar", lambda e: e.activation(out=cs[:].rearrange("p a b -> p (a b)"), in_=c2[:], func=AF.Silu),
          writes=[r_c2])
    P.add("vector", lambda e: e.tensor_copy(out=csb[:], in_=cs[:, :, 0:1].to_broadcast([128, 16, 128])),
          writes=[r_c2])
    wada_v = wview(w_ada)
    ada_bank = acc_bank

    def ada_fn(tag, buf, rr):
        kind, j0 = tag
        bk = ada_bank.next()
        if kind == "p":
            for cc in range(2):
                j = j0 + cc
                if cc == 1:
                    bk = ada_bank.next()
                for k in range(16):
                    P.add("tensor", lambda e, k=k, cc=cc, bk=bk: e.matmul(
                        bank32(bk)[:, 0:2], buf[:, k, cc * 128:(cc + 1) * 128], cs[:, k, :],
                        start=(k == 0), stop=(k == 15)), reads=rr + [r_c2], writes=[r_bank[bk]])
                addc = 1.0 if (16 <= j < 32 or 64 <= j < 80) else 0.0
                P.add("vector", lambda e, j=j, cc=cc, addc=addc, bk=bk: e.tensor_scalar(
                    out=modv[:, j, :], in0=bank32(bk)[:, 0:2], scalar1=badaT[:, j:j + 1],
                    scalar2=addc, op0=ALU.add, op1=ALU.add), reads=[r_bank[bk], r_c2], writes=[r_mod])
        else:
            gi, n0 = j0
            for k in range(16):
                P.add("tensor", lambda e, k=k: e.matmul(
                    bank32(bk)[:, 0:256], csb[:, k, :], buf[:, k, :], start=(k == 0), stop=(k == 15)),
                    reads=rr + [r_c2], writes=[r_bank[bk]])
            P.add("vector", lambda e: e.tensor_tensor(out=g_bc[gi][:, n0:n0 + 256], in0=bank32(bk)[:, 0:256],
                                                      in1=g_bc[gi][:, n0:n0 + 256], op=ALU.add),
                  reads=[r_bank[bk]], writes=[r_g[gi]])

    def ada_specs(chunks):
        return [(("p", j), (wada_v, 0, j * 128)) for j in chunks]

    stream(ada_specs(list(range(0, 32, 2))), ada_fn)
    ada_late = ada_specs(list(range(48, 80, 2)))
    for gi, base in ((0, 2 * D), (1, 5 * D)):
        for hh in range(2):
            P.add("gpsimd", lambda e, gi=gi, hh=hh: e.dma_start(
                out=g_bc[gi][:, hh * 1024:(hh + 1) * 1024],
                in_=b_ada_g[gi, hh * 1024:(hh + 1) * 1024].partition_broadcast(128)), writes=[r_g[gi]], dma=True)
        ada_late += [(("g", (gi, n0)), (wada_v, 0, base + n0)) for n0 in range(0, D, 256)]
    P.barrier()
    A.off = ph_mark

    hT = A.alloc("hT", [128, 16, 2304], BF16)
    hT_end = A.off
    r_hT = [Res("hT%d" % i) for i in range(5)]
    xin = [A.alloc("xin", [128, D], F32) for _ in range(3)]
    nrm = [A.alloc("nrm", [128, D], BF16) for _ in range(3)]
    r_xin = [Res("xin%d" % i) for i in range(3)]
    r_nrm = [Res("nrm%d" % i) for i in range(3)]
    stt = [A.alloc("stt", [128, 4, 6], F32) for _ in range(3)]
    mv = [A.alloc("mv", [128, 8], F32) for _ in range(3)]
    r_stat = [Res("stat%d" % i) for i in range(3)]
    gT_start = A.off
    gT = A.alloc("gT", [128, 8, 2080], BF16)
    r_gT = Res("gT")
    ropeb = A.alloc("ropeb", [128, 2, T], F32)
    r_rope = Res("rope")
    qb_p = Pool([(A.alloc("qb", [128, 512], BF16), Res("qb%d" % i)) for i in range(1)])
    t1_p = Pool([(A.alloc("t1", [128, 512], F32), Res("t1%d" % i)) for i in range(1)])
    t2_p = Pool([(A.alloc("t2", [128, 512], F32), Res("t2%d" % i)) for i in range(1)])
    qr_p = Pool([(A.alloc("qr", [128, 512], BF16), Res("qr%d" % i)) for i in range(2)])
    sg_p = Pool([(A.alloc("sg", [128, 512], BF16), Res("sg%d" % i)) for i in range(2)])
    vst_p = Pool([(A.alloc("vst", [128, 256], BF16), Res("vst%d" % i)) for i in range(2)])

    ln_i = [0]

    def ln_stats(src_ap_fn, stat, rsrc):
        stt_, mv_, rst_ = stat
        for c4 in range(4):
            P.add("vector", lambda e, c4=c4: e.bn_stats(out=stt_[:, c4, :], in_=src_ap_fn(c4)),
                  reads=[rsrc], writes=[rst_], loose=(c4 > 0))
        P.add("vector", lambda e: e.bn_aggr(out=mv_[:, 0:2], in_=stt_[:].rearrange("p a b -> p (a b)")),
              writes=[rst_])
        P.add("scalar", lambda e: e.activation(out=mv_[:, 2:3], in_=mv_[:, 1:2], func=AF.Sqrt, bias=EPS, scale=1.0),
              writes=[rst_])
        P.add("vector", lambda e: e.reciprocal(out=mv_[:, 3:4], in_=mv_[:, 2:3]), writes=[rst_])
        P.add("vector", lambda e: e.tensor_scalar(out=mv_[:, 4:5], in0=mv_[:, 0:1], scalar1=mv_[:, 3:4],
                                                  scalar2=-1.0, op0=ALU.mult, op1=ALU.mult), writes=[rst_])

    def transpose_mod(nrm_, rnrm_, tb, jsh, jsc, m, dst_fn, rdst):
        for k in range(16):
            bk = tb + k // 8
            P.add("tensor", lambda e, k=k, bk=bk: e.transpose(
                bank16(bk)[:, (k % 8) * 128:(k % 8 + 1) * 128], nrm_[:, k * 128:(k + 1) * 128], ident[:]),
                reads=[rnrm_, r_cst], writes=[r_bank[bk]])
        for k in range(16):
            bk = tb + k // 8
            src = bank16(bk)[:, (k % 8) * 128:(k % 8 + 1) * 128]
            if k < 8:
                P.add("scalar", lambda e, k=k, src=src: e.activation(
                    out=dst_fn(k), in_=src, func=AF.Identity, scale=modv[:, jsc + k, m:m + 1],
                    bias=modv[:, jsh + k, m:m + 1]), reads=[r_bank[bk], r_mod], writes=[rdst], loose=True)
            else:
                P.add("vector", lambda e, k=k, src=src: e.tensor_scalar(
                    out=dst_fn(k), in0=src, scalar1=modv[:, jsc + k, m:m + 1], scalar2=modv[:, jsh + k, m:m + 1],
                    op0=ALU.mult, op1=ALU.add), reads=[r_bank[bk], r_mod], writes=[rdst], loose=True)

    def ln_tiles(tiles):
        def stats(src, b):
            xb = xin[b]
            P.add("sync", lambda e: e.dma_start(out=xb[:], in_=src), writes=[r_xin[b]], dma=True)
            stt_, mv_, rst_ = stt[b], mv[b], r_stat[b]
            for c4 in range(4):
                P.add("vector", lambda e, c4=c4: e.bn_stats(out=stt_[:, c4, :], in_=xb[:, c4 * 512:(c4 + 1) * 512]),
                      reads=[r_xin[b]], writes=[rst_], loose=(c4 > 0))
            P.add("vector", lambda e: e.bn_aggr(out=mv_[:, 0:2], in_=stt_[:].rearrange("p a b -> p (a b)")),
                  writes=[rst_])

        def sqrt_(b):
            mv_, rst_ = mv[b], r_stat[b]
            P.add("scalar", lambda e: e.activation(out=mv_[:, 2:3], in_=mv_[:, 1:2], func=AF.Sqrt, bias=EPS, scale=1.0),
                  writes=[rst_])

        def rstd_norm(b):
            xb, nb, mv_, rst_ = xin[b], nrm[b], mv[b], r_stat[b]
            P.add("vector", lambda e: e.reciprocal(out=mv_[:, 3:4], in_=mv_[:, 2:3]), writes=[rst_])
            P.add("vector", lambda e: e.tensor_scalar(out=mv_[:, 4:5], in0=mv_[:, 0:1], scalar1=mv_[:, 3:4],
                                                      scalar2=-1.0, op0=ALU.mult, op1=ALU.mult), writes=[rst_])
            P.add("scalar", lambda e: e.activation(out=nb[:], in_=xb[:], func=AF.Identity,
                                                   scale=mv_[:, 3:4], bias=mv_[:, 4:5]),
                  reads=[r_xin[b], rst_], writes=[r_nrm[b]])

        def stage_b(m, col, b):
            transpose_mod(nrm[b], r_nrm[b], 2 * b, 0, 16, m, lambda k: hT[:, k, col:col + 128], r_hT[col // 512])
        n = len(tiles)
        base = ln_i[0]
        ln_i[0] += n
        stats(tiles[0][0], base % 3)
        sqrt_(base % 3)
        for step in range(n + 1):
            i1 = step
            i2 = step + 1
            i0 = step - 1
            if i1 < n:
                rstd_norm((base + i1) % 3)
            if i2 < n:
                stats(tiles[i2][0], (base + i2) % 3)
            if i0 >= 0:
                _, m, col = tiles[i0]
                stage_b(m, col, (base + i0) % 3)
            if i2 < n:
                sqrt_((base + i2) % 3)

    win_v = wview(w_in)

    def rope_store(ps_bank, rbank, tt, dst_ap):
        qb, rqb = qb_p.next()
        t1, rt1 = t1_p.next()
        t2, rt2 = t2_p.next()
        qr, rqr = qr_p.next()
        pqb = 4 + (tt % 2)
        P.add("scalar", lambda e: e.activation(out=qb[:], in_=bank32(ps_bank), func=AF.Identity), reads=[rbank], writes=[rqb])
        P.add("tensor", lambda e: e.matmul(bank32(pqb), permm[:], qb[:], start=True, stop=True),
              reads=[rqb, r_cst], writes=[r_bank[pqb]])
        P.add("vector", lambda e: e.tensor_tensor(out=t1[:], in0=bank32(ps_bank), in1=ropeb[:, 0, tt * 512:(tt + 1) * 512],
                                                  op=ALU.mult), reads=[rbank, r_rope, rqb], writes=[rt1])
        P.add("vector", lambda e: e.tensor_tensor(out=t2[:], in0=bank32(pqb), in1=ropeb[:, 1, tt * 512:(tt + 1) * 512],
                                                  op=ALU.mult), reads=[r_bank[pqb], r_rope], writes=[rt2])
        P.add("vector", lambda e: e.tensor_tensor(out=qr[:], in0=t1[:], in1=t2[:], op=ALU.add),
              reads=[rt1, rt2], writes=[rqr])
        P.add("sync", lambda e: e.dma_start(out=dst_ap, in_=qr[:]), reads=[rqr], dma=True)

    pend_rope = []

    def rope_defer(*args):
        pend_rope.append(args)
        if len(pend_rope) > 1:
            rope_store(*pend_rope.pop(0))

    def rope_flush():
        while pend_rope:
            rope_store(*pend_rope.pop(0))

    def proj_fm(buf, rr, cc, tok0, ntok, rh):
        bk = acc_bank.next()
        for k in range(16):
            P.add("tensor", lambda e, k=k: e.matmul(bank32(bk)[:, 0:ntok], buf[:, k, cc * 128:(cc + 1) * 128],
                                                    hT[:, k, tok0:tok0 + ntok], start=(k == 0), stop=(k == 15)),
                  reads=rr + rh, writes=[r_bank[bk]])
        return bk

    def v_tm(buf, rr, tile, chunk, h0, rh):
        bk = acc_bank.next()
        for k in range(16):
            P.add("tensor", lambda e, k=k: e.matmul(bank32(bk)[:, 0:256], hT[:, k, tile * 128:(tile + 1) * 128],
                                                    buf[:, k, :], start=(k == 0), stop=(k == 15)),
                  reads=rr + rh, writes=[r_bank[bk]])
        vst, rv = vst_p.next()
        P.add("scalar", lambda e: e.activation(out=vst[:], in_=bank32(bk)[:, 0:256], func=AF.Identity),
              reads=[r_bank[bk]], writes=[rv])
        P.add("sync", lambda e: e.dma_start(out=Vs[h0:h0 + 2, :, chunk, :].rearrange("h p e -> p h e"),
                                             in_=vst[:].rearrange("p (h e) -> p h e", h=2)), reads=[rv], dma=True)

    def glu_store(bk_a, bk_g, c, ncol, dst_ap, mask_ap=None):
        sg, rsg = sg_p.next()
        P.add("scalar", lambda e: e.activation(out=sg[:, 0:ncol], in_=bank32(bk_g)[:, 0:ncol], func=AF.Sigmoid,
                                               bias=bglu[:, 8 + c:9 + c], scale=1.0),
              reads=[r_bank[bk_g], r_cst], writes=[rsg])
        if mask_ap is None:
            P.add("vector", lambda e: e.scalar_tensor_tensor(out=dst_ap, in0=bank32(bk_a)[:, 0:ncol],
                                                             scalar=bglu[:, c:c + 1], in1=sg[:, 0:ncol],
                                                             op0=ALU.add, op1=ALU.mult),
                  reads=[r_bank[bk_a], rsg, r_cst], writes=[r_gT])
        else:
            P.add("vector", lambda e: e.scalar_tensor_tensor(out=sg[:, 0:ncol], in0=bank32(bk_a)[:, 0:ncol],
                                                             scalar=bglu[:, c:c + 1], in1=sg[:, 0:ncol],
                                                             op0=ALU.add, op1=ALU.mult),
                  reads=[r_bank[bk_a], r_cst], writes=[rsg])
            for hh in range(2):
                P.add("vector", lambda e, hh=hh: e.tensor_scalar(
                    out=dst_ap[hh], in0=sg[:, hh * 16:(hh + 1) * 16], scalar1=mask_ap[hh], scalar2=None, op0=ALU.mult),
                    reads=[rsg, r_cst], writes=[r_gT])

    xo_v = x_own.rearrange("(t p) d -> t p d", p=128)
    ln_tiles([(xo_v[t], 0, t * 128) for t in range(16)])
    P.add("sync", lambda e: e.dma_start(out=ropeb[:], in_=rope[0:2].rearrange("a p t -> p a t")), writes=[r_rope], dma=True)
    P.add("vector", lambda e: e.memset(gT[:].rearrange("p a b -> p (a b)"), 0.0), writes=[r_gT])

    rh_own = r_hT[0:4]

    def a2_fn(tag, buf, rr):
        kind, j = tag
        if kind in ("q", "k"):
            for cc in range(2):
                h = 2 * j + cc
                for tt in range(4):
                    bk = proj_fm(buf, rr, cc, tt * 512, 512, [r_hT[tt]])
                    dst = (Qs if kind == "q" else Ks)[h, :, tt * 512:(tt + 1) * 512]
                    rope_defer(bk, r_bank[bk], tt, dst)
            rope_flush()
        elif kind == "v":
            for tile in range(16):
                v_tm(buf, rr, tile, tile, 2 * j, [r_hT[tile // 4]])
        else:
            for tt in range(4):
                bk_a = proj_fm(buf, rr, 0, tt * 512, 512, [r_hT[tt]])
                bk_g = proj_fm(buf, rr, 1, tt * 512, 512, [r_hT[tt]])
                glu_store(bk_a, bk_g, j, 512, gT[:, j, 16 + tt * 512:16 + (tt + 1) * 512])

    specs = [(("q", j), (win_v, 0, j * 256)) for j in range(4)]
    specs += [(("k", j), (win_v, 0, 1024 + j * 256)) for j in range(4)]
    specs += [(("v", j), (win_v, 0, 2048 + j * 256)) for j in range(4)]
    specs += [(("u", c), (win_v, 0, 0, (3072 + c * 128, 4096 + c * 128))) for c in range(8)]
    merged = []
    for sp in specs:
        merged.append(sp)
        for _ in range(2):
            if ada_late:
                merged.append(ada_late.pop(0))
    merged += ada_late

    def a2_dispatch(tag, buf, rr):
        if tag[0] in ("p", "g"):
            ada_fn(tag, buf, rr)
        else:
            a2_fn(tag, buf, rr)
    stream(merged, a2_dispatch)

    xt_v = x_oth.rearrange("(t p) d -> t p d", p=128)
    cx_v = ctxb.rearrange("(t p) d -> t p d", p=128)
    ln_tiles([(xt_v[t], 0, t * 128) for t in range(16)] + [(cx_v[t], 1, 2048 + t * 128) for t in range(2)])
    P.add("sync", lambda e: e.dma_start(out=ropeb[:], in_=rope[2:4].rearrange("a p t -> p a t")), writes=[r_rope], dma=True)

    def a4_fn(tag, buf, rr):
        kind, j = tag
        if kind == "k":
            for cc in range(2):
                h = 2 * j + cc
                for tt in range(4):
                    bk = proj_fm(buf, rr, cc, tt * 512, 512, [r_hT[tt]])
                    rope_defer(bk, r_bank[bk], tt, Ks[h, :, 2048 + tt * 512:2048 + (tt + 1) * 512])
                bk = proj_fm(buf, rr, cc, 2048, 256, [r_hT[4]])
                rope_flush()
                qr, rqr = qr_p.next()
                P.add("scalar", lambda e, bk=bk, qr=qr: e.activation(out=qr[:, 0:256], in_=bank32(bk)[:, 0:256], func=AF.Identity),
                      reads=[r_bank[bk]], writes=[rqr])
                P.add("sync", lambda e, h=h, qr=qr: e.dma_start(out=Ks[h, :, 4096:4352], in_=qr[:, 0:256]), reads=[rqr], dma=True)
        elif kind == "v":
            for tile in range(18):
                v_tm(buf, rr, tile, 16 + tile, 2 * j, [r_hT[tile // 4]])
        else:
            bk_a = acc_bank.next()
            bk_g = acc_bank.next()
            for (bk, cc) in ((bk_a, 0), (bk_g, 1)):
                for hh, t0 in enumerate((0, 2032)):
                    for k in range(16):
                        P.add("tensor", lambda e, k=k, bk=bk, cc=cc, hh=hh, t0=t0: e.matmul(
                            bank32(bk)[:, hh * 16:(hh + 1) * 16], buf[:, k, cc * 128:(cc + 1) * 128],
                            hT[:, k, t0:t0 + 16], start=(k == 0), stop=(k == 15)),
                            reads=rr + [r_hT[0], r_hT[3]], writes=[r_bank[bk]])
            glu_store(bk_a, bk_g, j, 32, [gT[:, j, 2064:2080], gT[:, j, 0:16]], mask_ap=[hm[:, 1:2], hm[:, 0:1]])

    specs = [(("k", j), (win_v, 0, 1024 + j * 256)) for j in range(4)]
    specs += [(("v", j), (win_v, 0, 2048 + j * 256)) for j in range(4)]
    specs += [(("u", c), (win_v, 0, 0, (3072 + c * 128, 4096 + c * 128))) for c in range(8)]
    stream(specs, a4_fn)
    P.barrier()

    a5_mark = A.off
    A.off = ph_mark
    diag_all = A.alloc("diag", [128, 8, 31, 128], BF16)
    r_diag = [Res("diag%d" % i) for i in range(8)]
    c32 = A.alloc("c32", [128, 8, 512], F32)
    r_c32b = Res("c32b")
    csq_p = Pool([(A.alloc("csq", [128, 512], F32), Res("csq%d" % i)) for i in range(2)])
    meanb = A.alloc("meanb", [128, 512], F32)
    msq = A.alloc("msq", [128, 512], F32)
    rstdb = A.alloc("rstdb", [128, 512], F32)
    r_cstat = Res("cstat")
    ct_p = Pool([(A.alloc("ct", [128, 512], F32), Res("ct%d" % i)) for i in range(2)])
    co_p = Pool([(A.alloc("co", [128, 512], BF16), Res("co%d" % i)) for i in range(2)])
    assert A.off <= gT_start
    for c in range(8):
        for j in range(31):
            P.add("vector", lambda e, c=c, j=j: e.tensor_scalar(
                out=diag_all[:, c, j, :], in0=ident[:], scalar1=wdw[:, c * 31 + j:c * 31 + j + 1], scalar2=None,
                op0=ALU.mult), reads=[r_cst], writes=[r_diag[c]], loose=True)
    di = [0]
    pend_stats = []
    for tt in range(4):
        for c in range(8):
            bk = c % 2
            for j in range(31):
                P.add("tensor", lambda e, c=c, j=j, bk=bk, tt=tt: e.matmul(
                    bank32(bk), diag_all[:, c, j, :], gT[:, c, tt * 512 + j + 1:tt * 512 + j + 513],
                    start=(j == 0), stop=(j == 30)), reads=[r_diag[c], r_gT], writes=[r_bank[bk]])
            csq, rcsq = csq_p.next()
            P.add("scalar", lambda e, c=c, bk=bk: e.activation(out=c32[:, c, :], in_=bank32(bk), func=AF.Identity,
                                                               bias=cv[:, c:c + 1], scale=1.0),
                  reads=[r_bank[bk], r_cst], writes=[r_c32b], loose=True)
            P.add("scalar", lambda e, c=c, bk=bk, csq=csq: e.activation(out=csq[:], in_=bank32(bk), func=AF.Square,
                                                                       bias=cv[:, c:c + 1], scale=1.0),
                  reads=[r_bank[bk], r_cst], writes=[rcsq])
            def stats_mm(c=c, csq=csq, rcsq=rcsq):
                P.add("tensor", lambda e: e.matmul(bank32(4), ones32[:], c32[:, c, :], start=(c == 0), stop=(c == 7)),
                      reads=[r_c32b, r_cst], writes=[r_bank[4]])
                P.add("tensor", lambda e: e.matmul(bank32(5), ones32[:], csq[:], start=(c == 0), stop=(c == 7)),
                      reads=[rcsq, r_cst], writes=[r_bank[5]])
            pend_stats.append(stats_mm)
            if len(pend_stats) > 1:
                pend_stats.pop(0)()
        while pend_stats:
            pend_stats.pop(0)()
        P.add("scalar", lambda e: e.activation(out=meanb[:], in_=bank32(4), func=AF.Identity, scale=1.0 / 1024),
              reads=[r_bank[4]], writes=[r_cstat])
        P.add("vector", lambda e: e.tensor_tensor(out=msq[:], in0=meanb[:], in1=meanb[:], op=ALU.mult), writes=[r_cstat])
        P.add("vector", lambda e: e.scalar_tensor_tensor(out=msq[:], in0=bank32(5), scalar=1.0 / 1024, in1=msq[:],
                                                         op0=ALU.mult, op1=ALU.subtract), reads=[r_bank[5]], writes=[r_cstat])
        P.add("scalar", lambda e: e.activation(out=msq[:], in_=msq[:], func=AF.Sqrt, bias=EPS, scale=1.0), writes=[r_cstat])
        P.add("vector", lambda e: e.reciprocal(out=rstdb[:], in_=msq[:]), writes=[r_cstat])
        for c in range(8):
            ct, rct = ct_p.next()
            co, rco = co_p.next()
            P.add("vector", lambda e, c=c, ct=ct: e.tensor_tensor(out=ct[:], in0=c32[:, c, :], in1=meanb[:], op=ALU.subtract),
                  reads=[r_c32b, r_cstat], writes=[rct])
            P.add("vector", lambda e, ct=ct: e.tensor_tensor(out=ct[:], in0=ct[:], in1=rstdb[:], op=ALU.mult),
                  reads=[r_cstat], writes=[rct])
            P.add("scalar", lambda e, c=c, ct=ct, co=co: e.activation(out=co[:], in_=ct[:], func=AF.Silu,
                                                                      scale=cv[:, 8 + c:9 + c], bias=cv[:, 16 + c:17 + c]),
                  reads=[rct, r_cst], writes=[rco])
            P.add("sync", lambda e, c=c, co=co, tt=tt: e.dma_start(out=ACs[8 + c, :, tt * 512:(tt + 1) * 512], in_=co[:]),
                  reads=[rco], dma=True)
    P.barrier()
    A.off = ph_mark
    if stop_after == "A":
        return _finish(nc, P, out, A)

    Kh = [A.alloc("Kh", [128, NKEY], BF16) for _ in range(2)]
    Vh = [A.alloc("Vh", [128, 34, 130], BF16) for _ in range(2)]
    Qh = [A.alloc("Qh", [128, T], BF16) for _ in range(2)]
    r_k = [Res("k0"), Res("k1")]
    r_q = [Res("q0"), Res("q1")]
    r_v = [Res("v0"), Res("v1")]
    Pt_p = Pool([(A.alloc("Pt", [128, 1024], BF16), Res("Pt%d" % i)) for i in range(3)])
    attn = A.alloc("attn", [128, 16, 1024], BF16)
    r_attn = Res("attn")
    ob_p = Pool([(A.alloc("ob", [128, 128], F32), Res("ob%d" % i)) for i in range(2)])
    junk = A.alloc("junk", [128, 128], F32)
    rsb = A.alloc("rsb", [128, 16], F32)
    r_rs = Res("rs")
    aT_p = Pool([(A.alloc("aT", [128, 8, 128], BF16), Res("aT%d" % i)) for i in range(2)])
    for b in range(2):
        P.add("vector", lambda e, b=b: e.memset(Vh[b][:].rearrange("p a b -> p (a b)"), 1.0), writes=[r_v[b]])

    def load_head(h):
        b = h % 2
        P.add("sync", lambda e: e.dma_start(out=Kh[b][:], in_=Ks[h]), writes=[r_k[b]], dma=True)
        P.add("sync", lambda e: e.dma_start(out=Qh[b][:], in_=Qs[h]), writes=[r_q[b]], dma=True)
        P.add("sync", lambda e: e.dma_start(out=Vh[b][:, :, 0:128], in_=Vs[h]), writes=[r_v[b]], dma=True)

    SC = 0.125
    def ogrp(g):
        bk = 4 + g // 3
        off = (g % 3) * 160
        return bk, bank32(bk)[:, off:off + 129]

    oc_p = Pool([(A.alloc("oc", [128, 3, 512], F32), Res("oc%d" % i)) for i in range(2)])
    ob4 = [A.alloc("ob4", [128, 128], F32) for _ in range(4)]
    r_ob4 = [Res("ob4_%d" % i) for i in range(4)]
    iters = [(h, qb, kc) for h in range(8) for qb in range(4) for kc in range(34)]

    def emit_qk(i):
        h, qb, kc = iters[i]
        b = h % 2
        sb = 2 * (i % 2)
        K_, Q_ = Kh[b], Qh[b]
        for c in range(2):
            P.add("tensor", lambda e, c=c: e.matmul(
                bank32(sb + c), K_[c * 64:(c + 1) * 64, kc * 128:(kc + 1) * 128],
                Q_[c * 64:(c + 1) * 64, qb * 512:(qb + 1) * 512], start=True, stop=True),
                reads=[r_k[b], r_q[b]], writes=[r_bank[sb + c]])

    def emit_exp_pv(i):
        h, qb, kc = iters[i]
        b = h % 2
        sb = 2 * (i % 2)
        V_ = Vh[b]
        Pt, rPt = Pt_p.next()
        P.add("scalar", lambda e: e.activation(
            out=Pt[:], in_=psm[:, sb:sb + 2, :].rearrange("p a b -> p (a b)"), func=AF.Exp, scale=SC),
            reads=[r_bank[sb], r_bank[sb + 1]], writes=[rPt])
        for g in range(8):
            c, qs = g // 4, g % 4
            bk, oap = ogrp(g)
            P.add("tensor", lambda e, g=g, c=c, qs=qs, oap=oap: e.matmul(
                oap, Pt[:, c * 512 + qs * 128:c * 512 + (qs + 1) * 128], V_[:, kc, 0:129],
                start=(kc == 0 and g % 3 == 0), stop=(kc == 33), skip_group_check=True),
                reads=[rPt, r_v[b]], writes=[r_bank[bk]])

    def emit_evac(h, qb):
        oc, roc = oc_p.next()
        for g in range(8):
            j, o_ = g // 3, (g % 3) * 160
            P.add("vector", lambda e, j=j, o_=o_: e.tensor_copy(out=oc[:, j, o_:o_ + 129], in_=bank32(4 + j)[:, o_:o_ + 129]),
                  reads=[r_bank[4 + j]], writes=[roc], loose=(g > 0))
        for j in range(3):
            ng = 3 if j < 2 else 2
            P.add("vector", lambda e, j=j, ng=ng: e.reciprocal(
                out=rsb[:, 8 + 3 * j:8 + 3 * j + ng],
                in_=oc[:, j, 0:480].rearrange("p (g c) -> p g c", c=160)[:, 0:ng, 128:129]),
                reads=[roc], writes=[r_rs])
        P.add("vector", lambda e: e.tensor_scalar(out=rsb[:, 12:16], in0=rsb[:, 12:16], scalar1=lamv[:, 2:3], scalar2=None,
                                                  op0=ALU.mult), reads=[r_lam], writes=[r_rs])

        def og(g):
            return oc[:, g // 3, (g % 3) * 160:(g % 3) * 160 + 128]
        for qs in range(4):
            ob = ob4[qs]
            P.add("vector", lambda e, ob=ob, qs=qs: e.tensor_scalar(
                out=ob[:], in0=og(qs), scalar1=rsb[:, 8 + qs:9 + qs], scalar2=None, op0=ALU.mult),
                reads=[roc, r_rs], writes=[r_ob4[qs]])
            P.add("vector", lambda e, ob=ob, qs=qs: e.scalar_tensor_tensor(
                out=ob[:], in0=og(4 + qs), scalar=rsb[:, 12 + qs:13 + qs], in1=ob[:], op0=ALU.mult, op1=ALU.add),
                reads=[roc, r_rs], writes=[r_ob4[qs]])
            P.add("vector", lambda e, ob=ob, qs=qs: e.scalar_tensor_tensor(
                out=junk[:], in0=ob[:], scalar=1.0, in1=ob[:], op0=ALU.mult, op1=ALU.mult,
                accum_out=rsb[:, qs:qs + 1]), reads=[r_ob4[qs]], writes=[r_rs])
        def tail():
            P.add("scalar", lambda e: e.activation(out=rsb[:, 0:4], in_=rsb[:, 0:4], func=AF.Sqrt, bias=EPS, scale=1.0 / 128),
                  writes=[r_rs])
            P.add("vector", lambda e: e.reciprocal(out=rsb[:, 4:8], in_=rsb[:, 0:4]), writes=[r_rs])
            for qs in range(4):
                ob = ob4[qs]
                qt = qb * 4 + qs
                P.add("vector", lambda e, ob=ob, qs=qs, qt=qt: e.scalar_tensor_tensor(
                    out=attn[:, qt, h * 128:(h + 1) * 128], in0=ob[:], scalar=rsb[:, 4 + qs:5 + qs], in1=gsub[:],
                    op0=ALU.mult, op1=ALU.mult), reads=[r_ob4[qs], r_rs, r_cst], writes=[r_attn])
        return tail

    pend_tail = []
    load_head(0)
    load_head(1)
    emit_qk(0)
    for i, (h, qb, kc) in enumerate(iters):
        if i + 1 < len(iters):
            emit_qk(i + 1)
        emit_exp_pv(i)
        if kc == 3 and pend_tail:
            pend_tail.pop(0)()
        if kc == 33:
            pend_tail.append(emit_evac(h, qb))
            if qb == 3 and h + 2 < 8:
                load_head(h + 2)
    while pend_tail:
        pend_tail.pop(0)()
    for qt in range(16):
        bk = qt % 2
        aT, raT = aT_p.next()
        for k in range(8):
            P.add("tensor", lambda e, k=k, qt=qt, bk=bk: e.transpose(
                bank16(bk)[:, k * 128:(k + 1) * 128], attn[:, qt, k * 128:(k + 1) * 128], ident[:]),
                reads=[r_attn, r_cst], writes=[r_bank[bk]])
        P.add("vector", lambda e, aT=aT, bk=bk: e.tensor_copy(out=aT[:].rearrange("p a b -> p (a b)"), in_=bank16(bk)),
              reads=[r_bank[bk]], writes=[raT])
        P.add("sync", lambda e, aT=aT, qt=qt: e.dma_start(out=ACs[0:8, :, qt * 128:(qt + 1) * 128].rearrange("k p t -> p k t"),
                                                          in_=aT[:]), reads=[raT], dma=True)
    P.barrier()
    A.off = ph_mark
    if stop_after == "B":
        return _finish(nc, P, out, A)

    act = A.alloc("act", [128, 16, 512], BF16)
    r_act = Res("act")
    x1 = A.alloc("x1", [128, 4, D], F32)
    r_x1 = [Res("x1_%d" % i) for i in range(4)]
    hid = A.alloc("hid", [128, 64, 512], BF16)
    r_hid = Res("hid")
    nrmCs = [A.alloc("nrmc", [128, D], BF16) for _ in range(2)]
    r_nrmCs = [Res("nrmc0"), Res("nrmc1")]
    statCs = [(A.alloc("sttc", [128, 4, 6], F32), A.alloc("mvc", [128, 8], F32), Res("statc%d" % i)) for i in range(3)]
    stat_i = [0]

    def next_stat():
        stat_i[0] += 1
        return statCs[stat_i[0] % 3]
    lng = [A.alloc("ln1g", [128, D], F32), A.alloc("ln2g", [128, D], F32)]
    lnb = [A.alloc("ln1b", [128, D], BF16), A.alloc("ln2b", [128, D], BF16)]
    brow = [A.alloc("bo_row", [128, D], BF16), A.alloc("b2_row", [128, D], BF16)]
    r_ln = Res("ln")
    tmp_p = Pool([(A.alloc("tmpc", [128, 256], F32), Res("tmpc%d" % i)) for i in range(2)])
    rl_p = Pool([(A.alloc("rl", [128, 512], F32), Res("rl%d" % i)) for i in range(2)])
    for i, src in enumerate((ln1, ln2)):
        P.add("sync", lambda e, i=i, src=src: e.dma_start(out=lng[i][:], in_=src[0].partition_broadcast(128)), writes=[r_ln], dma=True)
        P.add("gpsimd", lambda e, i=i, src=src: e.dma_start(out=lnb[i][:, 0:1024], in_=src[1, 0:1024].partition_broadcast(128)),
              writes=[r_ln], dma=True)
        P.add("gpsimd", lambda e, i=i, src=src: e.dma_start(out=lnb[i][:, 1024:2048], in_=src[1, 1024:2048].partition_broadcast(128)),
              writes=[r_ln], dma=True)
    for i, src in enumerate((b_out, b_ff2)):
        P.add("vector", lambda e, i=i: e.memset(brow[i][:], 0.0), writes=[r_ln])
        for hh in range(2):
            P.add("gpsimd", lambda e, i=i, src=src, hh=hh: e.dma_start(out=brow[i][0:1, hh * 1024:(hh + 1) * 1024],
                                                                      in_=src[0:1, hh * 1024:(hh + 1) * 1024]),
                  writes=[r_ln], dma=True)
    wout_v = wview(w_out)
    w1_v = wview(w_ff1)
    w2_v = wview(w_ff2)
    out_v = out.rearrange("(t p) d -> t p d", p=128)
    acc6 = Pool([0, 1, 2, 3, 6, 7])

    def ln_apply(src_tile_fn, gi, dst_tile_fn, rsrc, rdst, rdst_halves=None):
        stC = next_stat()
        mvC = stC[1]
        ln_stats(lambda c4: src_tile_fn()[:, c4 * 512:(c4 + 1) * 512], stC, rsrc)
        P.add("scalar", lambda e: e.activation(out=dst_tile_fn(), in_=src_tile_fn(), func=AF.Identity,
                                               scale=mvC[:, 3:4], bias=mvC[:, 4:5]),
              reads=[rsrc, stC[2]], writes=[rdst])
        P.add("vector", lambda e: e.tensor_tensor(out=dst_tile_fn(), in0=dst_tile_fn(), in1=lng[gi][:], op=ALU.mult),
              reads=[r_ln], writes=[rdst])
        P.add("vector", lambda e: e.tensor_tensor(out=dst_tile_fn(), in0=dst_tile_fn(), in1=lnb[gi][:], op=ALU.add),
              reads=[r_ln], writes=[rdst])

    def load_act(tb):
        P.add("sync", lambda e: e.dma_start(out=act[:], in_=ACs[:, :, tb * 512:(tb + 1) * 512].rearrange("k p t -> p k t")),
              writes=[r_act], dma=True)

    load_act(0)
    for tb in range(4):
        for tt in range(4):
            P.add("sync", lambda e, tb=tb, tt=tt: e.dma_start(out=x1[:, tt, :], in_=xo_v[tb * 4 + tt]), writes=[r_x1[tt]], dma=True)

        def op_fn(tag, buf, rr, tb=tb):
            cb = tag
            for tt in range(4):
                bk = acc_bank.next()
                P.add("tensor", lambda e, bk=bk: e.matmul(bank32(bk)[:, 0:256], ones_bf[:], brow[0][:, cb * 256:(cb + 1) * 256],
                                                          start=True, stop=False), reads=[r_ln, r_cst], writes=[r_bank[bk]])
                for k in range(16):
                    P.add("tensor", lambda e, k=k, bk=bk, tt=tt: e.matmul(
                        bank32(bk)[:, 0:256], act[:, k, tt * 128:(tt + 1) * 128], buf[:, k, :], start=False, stop=(k == 15)),
                        reads=rr + [r_act], writes=[r_bank[bk]])
                tmp, rtmp = tmp_p.next()
                P.add("vector", lambda e, bk=bk, tmp=tmp: e.tensor_tensor(
                    out=tmp[:], in0=bank32(bk)[:, 0:256], in1=g_bc[0][:, cb * 256:(cb + 1) * 256],
                    op=ALU.mult), reads=[r_bank[bk], r_g[0]], writes=[rtmp])
                P.add("vector", lambda e, tt=tt, tmp=tmp: e.scalar_tensor_tensor(
                    out=x1[:, tt, cb * 256:(cb + 1) * 256], in0=x1[:, tt, cb * 256:(cb + 1) * 256], scalar=ALPHA,
                    in1=tmp[:], op0=ALU.mult, op1=ALU.add), reads=[rtmp], writes=[r_x1[tt]])
        stream([(cb, (wout_v, 0, cb * 256)) for cb in range(8)], op_fn)

        for tt in range(4):
            ln_apply(lambda tt=tt: x1[:, tt, :], 0, lambda tt=tt: x1[:, tt, :], r_x1[tt], r_x1[tt])
            stC = next_stat()
            nrmC, r_nrmC = nrmCs[tt % 2], r_nrmCs[tt % 2]
            ln_stats(lambda c4, tt=tt: x1[:, tt, c4 * 512:(c4 + 1) * 512], stC, r_x1[tt])
            P.add("scalar", lambda e, tt=tt, nrmC=nrmC, mvC=stC[1]: e.activation(
                out=nrmC[:], in_=x1[:, tt, :], func=AF.Identity, scale=mvC[:, 3:4], bias=mvC[:, 4:5]),
                reads=[r_x1[tt], stC[2]], writes=[r_nrmC])
            transpose_mod(nrmC, r_nrmC, 4 if tt % 2 == 0 else 6, 48, 64, 0,
                          lambda k, tt=tt: act[:, k, tt * 128:(tt + 1) * 128], r_act)

        def f1_fn(tag, buf, rr):
            jb = tag
            for cc in range(2):
                ch = jb * 2 + cc
                bk = acc6.next()
                for k in range(16):
                    P.add("tensor", lambda e, k=k, bk=bk, cc=cc: e.matmul(
                        bank32(bk), buf[:, k, cc * 128:(cc + 1) * 128], act[:, k, :], start=(k == 0), stop=(k == 15)),
                        reads=rr + [r_act], writes=[r_bank[bk]])
                rl, rrl = rl_p.next()
                P.add("vector", lambda e, bk=bk, ch=ch, rl=rl: e.tensor_scalar(
                    out=rl[:], in0=bank32(bk), scalar1=b1s[:, ch:ch + 1], scalar2=0.0, op0=ALU.add, op1=ALU.max),
                    reads=[r_bank[bk], r_cst], writes=[rrl])
                P.add("scalar", lambda e, ch=ch, rl=rl: e.activation(out=hid[:, ch, :], in_=rl[:], func=AF.Square),
                      reads=[rrl], writes=[r_hid], loose=True)
        stream([(jb, (w1_v, 0, jb * 256)) for jb in range(32)], f1_fn)
        if tb + 1 < 4:
            load_act(tb + 1)

        def f2_fn(tag, buf, rr):
            cb, jg = tag
            for tt in range(4):
                bk = tt
                if jg == 0:
                    P.add("tensor", lambda e, bk=bk: e.matmul(bank32(bk)[:, 0:256], ones_bf[:], brow[1][:, cb * 256:(cb + 1) * 256],
                                                              start=True, stop=False), reads=[r_ln, r_cst], writes=[r_bank[bk]])
                for j in range(16):
                    P.add("tensor", lambda e, j=j, bk=bk, tt=tt: e.matmul(
                        bank32(bk)[:, 0:256], hid[:, jg * 16 + j, tt * 128:(tt + 1) * 128], buf[:, j, :],
                        start=False, stop=(jg == 3 and j == 15)), reads=rr + [r_hid], writes=[r_bank[bk]])
                if jg == 3:
                    tmp, rtmp = tmp_p.next()
                    P.add("vector", lambda e, bk=bk, tmp=tmp: e.tensor_tensor(
                        out=tmp[:], in0=bank32(bk)[:, 0:256], in1=g_bc[1][:, cb * 256:(cb + 1) * 256], op=ALU.mult),
                        reads=[r_bank[bk], r_g[1]], writes=[rtmp])
                    P.add("vector", lambda e, tt=tt, tmp=tmp: e.scalar_tensor_tensor(
                        out=x1[:, tt, cb * 256:(cb + 1) * 256], in0=x1[:, tt, cb * 256:(cb + 1) * 256], scalar=ALPHA,
                        in1=tmp[:], op0=ALU.mult, op1=ALU.add), reads=[rtmp], writes=[r_x1[tt]])
        stream([((cb, jg), (w2_v, jg * 16, cb * 256)) for cb in range(8) for jg in range(4)], f2_fn)

        for tt in range(4):
            ln_apply(lambda tt=tt: x1[:, tt, :], 1, lambda tt=tt: x1[:, tt, :], r_x1[tt], r_x1[tt])
            P.add("sync", lambda e, tb=tb, tt=tt: e.dma_start(out=out_v[tb * 4 + tt], in_=x1[:, tt, :]), reads=[r_x1[tt]], dma=True)
    return _finish(nc, P, out, A)


def _finish(nc, P, out, A):
    P.emit()
    return nc, P


def _rope_tables(pos):
    row = (pos // 64).astype(np.float32)
    col = (pos % 64).astype(np.float32)
    inv = (10000.0 ** (-np.arange(0, 32, 2, dtype=np.float32) / 32.0)).astype(np.float32)
    ar = row[:, None] * inv[None, :]
    ac = col[:, None] * inv[None, :]
    ang = np.concatenate([ar, ar, ac, ac], axis=-1)
    cos = np.cos(ang).astype(np.float32)
    sin = np.sin(ang).astype(np.float32)
    sgn = np.where((np.arange(64) % 32) < 16, -1.0, 1.0).astype(np.float32)
    sin = sin * sgn[None, :]
    cosT = np.concatenate([cos.T, cos.T], axis=0)
    sinT = np.concatenate([sin.T, sin.T], axis=0)
    return np.ascontiguousarray(cosT), np.ascontiguousarray(sinT)


def _consts():
    c = np.zeros((128, 384), np.float32)
    c[:, 0:128] = np.eye(128, dtype=np.float32)
    pm = np.zeros((128, 128), np.float32)
    for m in range(128):
        d = m % 64
        pd = d + 16 if (d % 32) < 16 else d - 16
        pm[(m // 64) * 64 + pd, m] = 1.0
    c[:, 128:256] = pm
    c[:, 256:384] = 1.0
    return c


def make_in_maps(inp):
    f = lambda a: np.ascontiguousarray(np.asarray(a, dtype=np.float32))
    x = f(inp["x"]); c = f(inp["c"]); ctx = f(inp["ctx"]); c_ctx = f(inp["c_ctx"])
    w_ada = f(inp["w_ada"][0]); b_ada = f(inp["b_ada"][0]); w_in = f(inp["w_in"][0])
    b_glu = f(inp["b_glu"][0])
    shared = {
        "w_ada": w_ada, "w_in": w_in, "w_out": f(inp["w_out"][0]), "w_ff1": f(inp["w_ff1"][0]),
        "w_ff2": f(inp["w_ff2"][0]),
        "b_adaT": f(b_ada.reshape(96, 128).T),
        "b_ada_g": f(np.stack([b_ada[2 * D:3 * D], b_ada[5 * D:6 * D]])),
        "b_gluT": f(b_glu.reshape(16, 128).T),
        "lam4": f(np.stack([inp["lambda_q1"][0], inp["lambda_k1"][0], inp["lambda_q2"][0], inp["lambda_k2"][0]])),
        "subg": f(inp["subln_g"][0].reshape(1, 128)),
        "w_dwT": f(np.asarray(inp["w_dw"][0]).reshape(31, 8, 128).transpose(2, 1, 0).reshape(128, 8 * 31)),
        "cvec": f(np.concatenate([np.asarray(inp[k][0]).reshape(8, 128).T for k in ("b_dw", "conv_ln_g", "conv_ln_b")], axis=1)),
        "b_out": f(inp["b_out"][0].reshape(1, D)),
        "ln1": f(np.stack([inp["ln1_g"][0], inp["ln1_b"][0]])),
        "b1T": f(np.asarray(inp["b_ff1"][0]).reshape(64, 128).T),
        "b_ff2": f(inp["b_ff2"][0].reshape(1, D)),
        "ln2": f(np.stack([inp["ln2_g"][0], inp["ln2_b"][0]])),
        "consts": _consts(),
    }
    maps = []
    for core in range(8):
        b, s = core // 2, core % 2
        own = np.arange(s * T, (s + 1) * T)
        oth = np.arange((1 - s) * T, (2 - s) * T)
        co, so = _rope_tables(own)
        ct, st_ = _rope_tables(oth)
        hm = np.zeros((128, 2), np.float32)
        hm[:, 0] = 1.0 if s == 1 else 0.0
        hm[:, 1] = 1.0 if s == 0 else 0.0
        m = dict(shared)
        m.update({
            "x_own": f(x[b, own]), "x_oth": f(x[b, oth]), "ctxb": f(ctx[b]),
            "cT2": f(np.stack([c[b].reshape(16, 128).T, c_ctx.reshape(16, 128).T], axis=-1).reshape(128, 32)),
            "rope": f(np.stack([co, so, ct, st_])),
            "hmask": hm,
        })
        maps.append(m)
    return maps


_CACHE = {}


def kernel(**inputs):
    if "nc" not in _CACHE:
        _CACHE["nc"] = build_program()[0]
    nc = _CACHE["nc"]
    maps = make_in_maps(inputs)
    res = run_bass_kernel_spmd(nc, maps, core_ids=list(range(8)))
    outp = np.empty((4, L, D), np.float32)
    for core in range(8):
        b, s = core // 2, core % 2
        outp[b, s * T:(s + 1) * T] = np.asarray(res.results[core]["out"], dtype=np.float32)
    return outp
```

```python
import contextlib
import math
import numpy as np
import concourse.bass as bass
import concourse.mybir as mybir
from concourse.bass_utils import run_bass_kernel_spmd

F32 = mybir.dt.float32
BF16 = mybir.dt.bfloat16
AF = mybir.ActivationFunctionType
ALU = mybir.AluOpType
ENGS = ["sync", "scalar", "vector", "gpsimd", "tensor"]

D = 2048
L = 4096
T = 2048
CTX = 256
NKEY = 4352
EPS = 1e-5
ALPHA = 2.0 ** 0.25
LAM_INIT = 0.2


class Res:
    __slots__ = ("name", "last_w", "readers")

    def __init__(self, name):
        self.name = name
        self.last_w = None
        self.readers = []


class Op:
    __slots__ = ("eng", "fn", "deps", "dma", "signal", "event", "idx")


class Prog:
    def __init__(self, nc, n_dma_sems=16):
        self.nc = nc
        self.ops = []
        self.n_dma_sems = n_dma_sems
        self.last_op = {e: None for e in ENGS}
        self.pend = {e: [] for e in ENGS}
        self.dmas_since = []

    def add(self, eng, fn, reads=(), writes=(), dma=False, loose=False):
        op = Op()
        op.eng = eng
        op.fn = fn
        op.dma = dma
        op.signal = dma
        op.event = None
        op.idx = len(self.ops)
        deps = list(self.pend[eng])
        self.pend[eng] = []
        for r in reads:
            if r.last_w is not None:
                deps.append(r.last_w)
            if dma:
                r.readers.append(op)
            else:
                r.readers = [o for o in r.readers if o.dma or o.eng != eng]
                r.readers.append(op)
        for w in writes:
            if w.last_w is not None:
                deps.append(w.last_w)
            deps.extend(w.readers)
            w.last_w = op
            w.readers = []
        dd = []
        seen = set()
        for d in deps:
            if d is op or d.idx in seen:
                continue
            if eng == "tensor" and d.eng == "tensor" and not d.dma:
                continue
            if loose and d.eng == eng and not d.dma:
                continue
            seen.add(d.idx)
            dd.append(d)
            d.signal = True
        op.deps = dd
        self.ops.append(op)
        self.last_op[eng] = op
        if dma:
            self.dmas_since.append(op)
        return op

    def barrier(self):
        lst = [o for o in self.last_op.values() if o is not None] + self.dmas_since
        self.dmas_since = []
        for e in ENGS:
            self.pend[e] = list(lst)
        for o in lst:
            o.signal = True

    def emit(self, final_wait_eng="sync"):
        nc = self.nc
        with contextlib.ExitStack() as st:
            engsem = {e: st.enter_context(nc.semaphore("c_" + e)) for e in ENGS}
            dsem = {e: [st.enter_context(nc.semaphore("d_%s_%d" % (e, i))) for i in range(self.n_dma_sems)]
                    for e in ("sync", "scalar", "gpsimd")}
            cnt = {e: 0 for e in ENGS}
            dval = {e: [0] * self.n_dma_sems for e in dsem}
            dlast = {e: [None] * self.n_dma_sems for e in dsem}
            drr = {e: 0 for e in dsem}
            for op in self.ops:
                if op.dma:
                    e = op.eng
                    i = drr[e]
                    drr[e] = (i + 1) % self.n_dma_sems
                    prev = dlast[e][i]
                    if prev is not None:
                        op.deps.append(prev)
                    dval[e][i] += 16
                    op.event = (dsem[e][i], dval[e][i], ("d", e, i))
                    dlast[e][i] = op
                elif op.signal:
                    cnt[op.eng] += 1
                    op.event = (engsem[op.eng], cnt[op.eng], ("c", op.eng))
            tail = [op for op in self.ops if op.dma]
            streams = {e: [o for o in self.ops if o.eng == e] for e in ENGS}
            self.stats = {e: len(streams[e]) for e in ENGS}
            self.cnts = dict(cnt)
            block = st.enter_context(nc.Block())

            def run_stream(ename, eng):
                waited = {}
                for op in streams[ename]:
                    for d in op.deps:
                        sem, val, key = d.event
                        if waited.get(key, 0) < val:
                            eng.wait_ge(sem, val)
                            waited[key] = val
                    ins = op.fn(eng)
                    if op.event is not None:
                        sem, val, key = op.event
                        ins.then_inc(sem, 16 if op.dma else 1)
                if ename == final_wait_eng:
                    mx = {}
                    for op in tail:
                        sem, val, key = op.event
                        if key not in mx or mx[key][1] < val:
                            mx[key] = (sem, val)
                    for key, (sem, val) in mx.items():
                        if waited.get(key, 0) < val:
                            eng.wait_ge(sem, val)
                            waited[key] = val

            @block.sync
            def _(eng):
                run_stream("sync", eng)

            @block.scalar
            def _(eng):
                run_stream("scalar", eng)

            @block.vector
            def _(eng):
                run_stream("vector", eng)

            @block.gpsimd
            def _(eng):
                run_stream("gpsimd", eng)

            @block.tensor
            def _(eng):
                run_stream("tensor", eng)


class Arena:
    def __init__(self, nc, limit):
        self.nc = nc
        self.off = 16512
        self.limit = limit
        self.n = 0

    def alloc(self, name, shape, dt):
        esz = 2 if dt == BF16 else 4
        size = esz * int(np.prod(shape[1:]))
        size = (size + 31) // 32 * 32
        self.n += 1
        t = self.nc.alloc_sbuf_tensor_at("%s_%d" % (name, self.n), list(shape), dt, offset=self.off)
        self.off += size
        assert self.off <= self.limit, (name, self.off, self.limit)
        return t


class Pool:
    def __init__(self, items):
        self.items = items
        self.i = 0

    def next(self):
        it = self.items[self.i % len(self.items)]
        self.i += 1
        return it


def build_program(debug=False, stop_after="D"):
    nc = bass.Bass("TRN2", target_bir_lowering=False)
    P = Prog(nc)

    def din(name, shape):
        return nc.dram_tensor(name, list(shape), F32, kind="ExternalInput").ap()

    x_own = din("x_own", [T, D])
    x_oth = din("x_oth", [T, D])
    ctxb = din("ctxb", [CTX, D])
    cT2 = din("cT2", [128, 32])
    w_ada = din("w_ada", [D, 6 * D])
    w_in = din("w_in", [D, 5120])
    w_out = din("w_out", [D, D])
    w_ff1 = din("w_ff1", [D, 4 * D])
    w_ff2 = din("w_ff2", [4 * D, D])
    b_adaT = din("b_adaT", [128, 96])
    b_ada_g = din("b_ada_g", [2, D])
    b_gluT = din("b_gluT", [128, 16])
    lam4 = din("lam4", [4, 64])
    subg = din("subg", [1, 128])
    w_dwT = din("w_dwT", [128, 8 * 31])
    cvec = din("cvec", [128, 24])
    b_out = din("b_out", [1, D])
    ln1 = din("ln1", [2, D])
    b1T = din("b1T", [128, 64])
    b_ff2 = din("b_ff2", [1, D])
    ln2 = din("ln2", [2, D])
    rope = din("rope", [4, 128, T])
    hmask = din("hmask", [128, 2])
    consts = din("consts", [128, 384])
    out = nc.dram_tensor("out", [T, D], F32, kind="ExternalOutput").ap()
    skind = "ExternalOutput" if debug else "Internal"
    Qs = nc.dram_tensor("Qs", [8, 128, T], BF16, kind=skind).ap()
    Ks = nc.dram_tensor("Ks", [8, 128, NKEY], BF16, kind=skind).ap()
    Vs = nc.dram_tensor("Vs", [8, 128, 34, 128], BF16, kind=skind).ap()
    ACs = nc.dram_tensor("ACs", [16, 128, T], BF16, kind=skind).ap()

    A = Arena(nc, 229344)
    psm = nc.alloc_psum_tensor("psm", [128, 6, 512], F32)
    pst = nc.alloc_psum_tensor("pst", [128, 2, 1024], BF16)
    r_bank = [Res("bank%d" % i) for i in range(8)]

    def bank32(i):
        return psm[:, i, :] if i < 6 else pst[:, i - 6, :].bitcast(F32)

    def bank16(i):
        return psm[:, i, :].bitcast(BF16) if i < 6 else pst[:, i - 6, :]

    ident = A.alloc("ident", [128, 128], BF16)
    permm = A.alloc("permm", [128, 128], BF16)
    ones_bf = A.alloc("ones_bf", [128, 128], BF16)
    ones32 = A.alloc("ones32", [128, 128], F32)
    modv = A.alloc("modv", [128, 96, 2], F32)
    g_bc = [A.alloc("g1bc", [128, D], BF16), A.alloc("g2bc", [128, D], BF16)]
    bglu = A.alloc("bglu", [128, 16], F32)
    cv = A.alloc("cv", [128, 24], F32)
    b1s = A.alloc("b1s", [128, 64], F32)
    hm = A.alloc("hm", [128, 2], F32)
    wdw = A.alloc("wdw", [128, 8 * 31], F32)
    lamv = A.alloc("lamv", [128, 8], F32)
    gsub = A.alloc("gsub", [128, 128], F32)
    wbufs = [A.alloc("wblk", [128, 16, 256], BF16) for _ in range(3)]
    wpool = Pool([(wbufs[i], [Res("w%da" % i), Res("w%db" % i)]) for i in range(3)])
    c2 = A.alloc("c2", [128, 32], F32)
    cs = A.alloc("cs", [128, 16, 2], BF16)
    csb = A.alloc("csb", [128, 16, 128], BF16)
    badaT = A.alloc("badaT", [128, 96], F32)
    acc_bank = Pool([0, 1, 2, 3])
    r_cst = Res("cst")
    r_mod = Res("mod")
    r_g = [Res("g1"), Res("g2")]
    r_lam = Res("lam")
    persist_end = A.off

    def wview(W):
        return W.rearrange("(kc p) n -> p kc n", p=128)

    def load_w(Wv, kc0, n0, halves=None):
        buf, rr = wpool.next()
        if halves is None:
            P.add("gpsimd", lambda e: e.dma_start(out=buf[:], in_=Wv[:, kc0:kc0 + 16, n0:n0 + 256]),
                  writes=rr, dma=True)
        else:
            for hh, nn in enumerate(halves):
                P.add("gpsimd", lambda e, hh=hh, nn=nn: e.dma_start(
                    out=buf[:, :, hh * 128:(hh + 1) * 128], in_=Wv[:, kc0:kc0 + 16, nn:nn + 128]),
                    writes=[rr[hh]], dma=True)
        return buf, rr

    def stream(specs, fn, depth=2):
        loaded = []
        n = len(specs)
        for i in range(min(depth, n)):
            loaded.append(load_w(*specs[i][1]))
        for i in range(n):
            buf, rr = loaded[i]
            fn(specs[i][0], buf, rr)
            if i + depth < n:
                loaded.append(load_w(*specs[i + depth][1]))

    ph_mark = A.off
    cst32 = A.alloc("cst32", [128, 384], F32)
    r_c32 = Res("c32")
    P.add("sync", lambda e: e.dma_start(out=cst32[:], in_=consts), writes=[r_c32], dma=True)
    P.add("vector", lambda e: e.tensor_copy(out=ident[:], in_=cst32[:, 0:128]), reads=[r_c32], writes=[r_cst])
    P.add("vector", lambda e: e.tensor_copy(out=permm[:], in_=cst32[:, 128:256]), reads=[r_c32], writes=[r_cst])
    P.add("vector", lambda e: e.tensor_copy(out=ones_bf[:], in_=cst32[:, 256:384]), reads=[r_c32], writes=[r_cst])
    P.add("vector", lambda e: e.tensor_copy(out=ones32[:], in_=cst32[:, 256:384]), reads=[r_c32], writes=[r_cst])
    for dst, src in ((bglu, b_gluT), (cv, cvec), (b1s, b1T), (hm, hmask), (wdw, w_dwT)):
        P.add("sync", lambda e, dst=dst, src=src: e.dma_start(out=dst[:], in_=src), writes=[r_cst], dma=True)
    P.add("sync", lambda e: e.dma_start(out=gsub[:], in_=subg.partition_broadcast(128).rearrange("p a b -> p (a b)")),
          writes=[r_cst], dma=True)
    lam_in = A.alloc("lam_in", [128, 4, 64], F32)
    lam_t = A.alloc("lam_t", [128, 2, 64], F32)
    P.add("sync", lambda e: e.dma_start(out=lam_in[:].rearrange("p a b -> p (a b)"),
                                         in_=lam4.rearrange("a b -> (a b)").partition_broadcast(128)),
          writes=[r_lam], dma=True)
    P.add("vector", lambda e: e.tensor_tensor(out=lam_t[:, 0, :], in0=lam_in[:, 0, :], in1=lam_in[:, 1, :], op=ALU.mult),
          writes=[r_lam])
    P.add("vector", lambda e: e.tensor_tensor(out=lam_t[:, 1, :], in0=lam_in[:, 2, :], in1=lam_in[:, 3, :], op=ALU.mult),
          writes=[r_lam])
    P.add("vector", lambda e: e.reduce_sum(out=lamv[:, 0:2], in_=lam_t[:], axis=mybir.AxisListType.X), writes=[r_lam])
    P.add("scalar", lambda e: e.activation(out=lamv[:, 3:5], in_=lamv[:, 0:2], func=AF.Exp), writes=[r_lam])
    P.add("vector", lambda e: e.scalar_tensor_tensor(out=lamv[:, 2:3], in0=lamv[:, 4:5], scalar=-LAM_INIT,
                                                     in1=lamv[:, 3:4], op0=ALU.add, op1=ALU.subtract), writes=[r_lam])
    P.add("vector", lambda e: e.tensor_scalar(out=gsub[:], in0=gsub[:], scalar1=1.0 - LAM_INIT, scalar2=None,
                                              op0=ALU.mult), writes=[r_cst])

    r_c2 = Res("c2")
    P.add("sync", lambda e: e.dma_start(out=c2[:], in_=cT2), writes=[r_c2], dma=True)
    P.add("sync", lambda e: e.dma_start(out=badaT[:], in_=b_adaT), writes=[r_c2], dma=True)
    P.add("scalar", lambda e: e.activation(out=cs[:].rearrange("p a b -> p (a b)"), in_=c2[:], func=AF.Silu),
          writes=[r_c2])
    P.add("vector", lambda e: e.tensor_copy(out=csb[:], in_=cs[:, :, 0:1].to_broadcast([128, 16, 128])),
          writes=[r_c2])
    wada_v = wview(w_ada)
    ada_bank = acc_bank

    def ada_fn(tag, buf, rr):
        kind, j0 = tag
        bk = ada_bank.next()
        if kind == "p":
            for cc in range(2):
                j = j0 + cc
                if cc == 1:
                    bk = ada_bank.next()
                for k in range(16):
                    P.add("tensor", lambda e, k=k, cc=cc, bk=bk: e.matmul(
                        bank32(bk)[:, 0:2], buf[:, k, cc * 128:(cc + 1) * 128], cs[:, k, :],
                        start=(k == 0), stop=(k == 15)), reads=rr + [r_c2], writes=[r_bank[bk]])
                addc = 1.0 if (16 <= j < 32 or 64 <= j < 80) else 0.0
                P.add("vector", lambda e, j=j, cc=cc, addc=addc, bk=bk: e.tensor_scalar(
                    out=modv[:, j, :], in0=bank32(bk)[:, 0:2], scalar1=badaT[:, j:j + 1],
                    scalar2=addc, op0=ALU.add, op1=ALU.add), reads=[r_bank[bk], r_c2], writes=[r_mod])
        else:
            gi, n0 = j0
            for k in range(16):
                P.add("tensor", lambda e, k=k: e.matmul(
                    bank32(bk)[:, 0:256], csb[:, k, :], buf[:, k, :], start=(k == 0), stop=(k == 15)),
                    reads=rr + [r_c2], writes=[r_bank[bk]])
            P.add("vector", lambda e: e.tensor_tensor(out=g_bc[gi][:, n0:n0 + 256], in0=bank32(bk)[:, 0:256],
                                                      in1=g_bc[gi][:, n0:n0 + 256], op=ALU.add),
                  reads=[r_bank[bk]], writes=[r_g[gi]])

    def ada_specs(chunks):
        return [(("p", j), (wada_v, 0, j * 128)) for j in chunks]

    stream(ada_specs(list(range(0, 32, 2))), ada_fn)
    ada_late = ada_specs(list(range(48, 80, 2)))
    for gi, base in ((0, 2 * D), (1, 5 * D)):
        for hh in range(2):
            P.add("gpsimd", lambda e, gi=gi, hh=hh: e.dma_start(
                out=g_bc[gi][:, hh * 1024:(hh + 1) * 1024],
                in_=b_ada_g[gi, hh * 1024:(hh + 1) * 1024].partition_broadcast(128)), writes=[r_g[gi]], dma=True)
        ada_late += [(("g", (gi, n0)), (wada_v, 0, base + n0)) for n0 in range(0, D, 256)]
    P.barrier()
    A.off = ph_mark

    hT = A.alloc("hT", [128, 16, 2304], BF16)
    hT_end = A.off
    r_hT = [Res("hT%d" % i) for i in range(5)]
    xin = [A.alloc("xin", [128, D], F32) for _ in range(3)]
    nrm = [A.alloc("nrm", [128, D], BF16) for _ in range(3)]
    r_xin = [Res("xin%d" % i) for i in range(3)]
    r_nrm = [Res("nrm%d" % i) for i in range(3)]
    stt = [A.alloc("stt", [128, 4, 6], F32) for _ in range(3)]
    mv = [A.alloc("mv", [128, 8], F32) for _ in range(3)]
    r_stat = [Res("stat%d" % i) for i in range(3)]
    gT_start = A.off
    gT = A.alloc("gT", [128, 8, 2080], BF16)
    r_gT = Res("gT")
    ropeb = A.alloc("ropeb", [128, 2, T], F32)
    r_rope = Res("rope")
    qb_p = Pool([(A.alloc("qb", [128, 512], BF16), Res("qb%d" % i)) for i in range(1)])
    t1_p = Pool([(A.alloc("t1", [128, 512], F32), Res("t1%d" % i)) for i in range(1)])
    t2_p = Pool([(A.alloc("t2", [128, 512], F32), Res("t2%d" % i)) for i in range(1)])
    qr_p = Pool([(A.alloc("qr", [128, 512], BF16), Res("qr%d" % i)) for i in range(2)])
    sg_p = Pool([(A.alloc("sg", [128, 512], BF16), Res("sg%d" % i)) for i in range(2)])
    vst_p = Pool([(A.alloc("vst", [128, 256], BF16), Res("vst%d" % i)) for i in range(2)])

    ln_i = [0]

    def ln_stats(src_ap_fn, stat, rsrc):
        stt_, mv_, rst_ = stat
        for c4 in range(4):
            P.add("vector", lambda e, c4=c4: e.bn_stats(out=stt_[:, c4, :], in_=src_ap_fn(c4)),
                  reads=[rsrc], writes=[rst_], loose=(c4 > 0))
        P.add("vector", lambda e: e.bn_aggr(out=mv_[:, 0:2], in_=stt_[:].rearrange("p a b -> p (a b)")),
              writes=[rst_])
        P.add("scalar", lambda e: e.activation(out=mv_[:, 2:3], in_=mv_[:, 1:2], func=AF.Sqrt, bias=EPS, scale=1.0),
              writes=[rst_])
        P.add("vector", lambda e: e.reciprocal(out=mv_[:, 3:4], in_=mv_[:, 2:3]), writes=[rst_])
        P.add("vector", lambda e: e.tensor_scalar(out=mv_[:, 4:5], in0=mv_[:, 0:1], scalar1=mv_[:, 3:4],
                                                  scalar2=-1.0, op0=ALU.mult, op1=ALU.mult), writes=[rst_])

    def transpose_mod(nrm_, rnrm_, tb, jsh, jsc, m, dst_fn, rdst):
        for k in range(16):
            bk = tb + k // 8
            P.add("tensor", lambda e, k=k, bk=bk: e.transpose(
                bank16(bk)[:, (k % 8) * 128:(k % 8 + 1) * 128], nrm_[:, k * 128:(k + 1) * 128], ident[:]),
                reads=[rnrm_, r_cst], writes=[r_bank[bk]])
        for k in range(16):
            bk = tb + k // 8
            src = bank16(bk)[:, (k % 8) * 128:(k % 8 + 1) * 128]
            if k < 8:
                P.add("scalar", lambda e, k=k, src=src: e.activation(
                    out=dst_fn(k), in_=src, func=AF.Identity, scale=modv[:, jsc + k, m:m + 1],
                    bias=modv[:, jsh + k, m:m + 1]), reads=[r_bank[bk], r_mod], writes=[rdst], loose=True)
            else:
                P.add("vector", lambda e, k=k, src=src: e.tensor_scalar(
                    out=dst_fn(k), in0=src, scalar1=modv[:, jsc + k, m:m + 1], scalar2=modv[:, jsh + k, m:m + 1],
                    op0=ALU.mult, op1=ALU.add), reads=[r_bank[bk], r_mod], writes=[rdst], loose=True)

    def ln_tiles(tiles):
        def stats(src, b):
            xb = xin[b]
            P.add("sync", lambda e: e.dma_start(out=xb[:], in_=src), writes=[r_xin[b]], dma=True)
            stt_, mv_, rst_ = stt[b], mv[b], r_stat[b]
            for c4 in range(4):
                P.add("vector", lambda e, c4=c4: e.bn_stats(out=stt_[:, c4, :], in_=xb[:, c4 * 512:(c4 + 1) * 512]),
                      reads=[r_xin[b]], writes=[rst_], loose=(c4 > 0))
            P.add("vector", lambda e: e.bn_aggr(out=mv_[:, 0:2], in_=stt_[:].rearrange("p a b -> p (a b)")),
                  writes=[rst_])

        def sqrt_(b):
            mv_, rst_ = mv[b], r_stat[b]
            P.add("scalar", lambda e: e.activation(out=mv_[:, 2:3], in_=mv_[:, 1:2], func=AF.Sqrt, bias=EPS, scale=1.0),
                  writes=[rst_])

        def rstd_norm(b):
            xb, nb, mv_, rst_ = xin[b], nrm[b], mv[b], r_stat[b]
            P.add("vector", lambda e: e.reciprocal(out=mv_[:, 3:4], in_=mv_[:, 2:3]), writes=[rst_])
            P.add("vector", lambda e: e.tensor_scalar(out=mv_[:, 4:5], in0=mv_[:, 0:1], scalar1=mv_[:, 3:4],
                                                      scalar2=-1.0, op0=ALU.mult, op1=ALU.mult), writes=[rst_])
            P.add("scalar", lambda e: e.activation(out=nb[:], in_=xb[:], func=AF.Identity,
                                                   scale=mv_[:, 3:4], bias=mv_[:, 4:5]),
                  reads=[r_xin[b], rst_], writes=[r_nrm[b]])

        def stage_b(m, col, b):
            transpose_mod(nrm[b], r_nrm[b], 2 * b, 0, 16, m, lambda k: hT[:, k, col:col + 128], r_hT[col // 512])
        n = len(tiles)
        base = ln_i[0]
        ln_i[0] += n
        stats(tiles[0][0], base % 3)
        sqrt_(base % 3)
        for step in range(n + 1):
            i1 = step
            i2 = step + 1
            i0 = step - 1
            if i1 < n:
                rstd_norm((base + i1) % 3)
            if i2 < n:
                stats(tiles[i2][0], (base + i2) % 3)
            if i0 >= 0:
                _, m, col = tiles[i0]
                stage_b(m, col, (base + i0) % 3)
            if i2 < n:
                sqrt_((base + i2) % 3)

    win_v = wview(w_in)

    def rope_store(ps_bank, rbank, tt, dst_ap):
        qb, rqb = qb_p.next()
        t1, rt1 = t1_p.next()
        t2, rt2 = t2_p.next()
        qr, rqr = qr_p.next()
        pqb = 4 + (tt % 2)
        P.add("scalar", lambda e: e.activation(out=qb[:], in_=bank32(ps_bank), func=AF.Identity), reads=[rbank], writes=[rqb])
        P.add("tensor", lambda e: e.matmul(bank32(pqb), permm[:], qb[:], start=True, stop=True),
              reads=[rqb, r_cst], writes=[r_bank[pqb]])
        P.add("vector", lambda e: e.tensor_tensor(out=t1[:], in0=bank32(ps_bank), in1=ropeb[:, 0, tt * 512:(tt + 1) * 512],
                                                  op=ALU.mult), reads=[rbank, r_rope, rqb], writes=[rt1])
        P.add("vector", lambda e: e.tensor_tensor(out=t2[:], in0=bank32(pqb), in1=ropeb[:, 1, tt * 512:(tt + 1) * 512],
                                                  op=ALU.mult), reads=[r_bank[pqb], r_rope], writes=[rt2])
        P.add("vector", lambda e: e.tensor_tensor(out=qr[:], in0=t1[:], in1=t2[:], op=ALU.add),
              reads=[rt1, rt2], writes=[rqr])
        P.add("sync", lambda e: e.dma_start(out=dst_ap, in_=qr[:]), reads=[rqr], dma=True)

    pend_rope = []

    def rope_defer(*args):
        pend_rope.append(args)
        if len(pend_rope) > 1:
            rope_store(*pend_rope.pop(0))

    def rope_flush():
        while pend_rope:
            rope_store(*pend_rope.pop(0))

    def proj_fm(buf, rr, cc, tok0, ntok, rh):
        bk = acc_bank.next()
        for k in range(16):
            P.add("tensor", lambda e, k=k: e.matmul(bank32(bk)[:, 0:ntok], buf[:, k, cc * 128:(cc + 1) * 128],
                                                    hT[:, k, tok0:tok0 + ntok], start=(k == 0), stop=(k == 15)),
                  reads=rr + rh, writes=[r_bank[bk]])
        return bk

    def v_tm(buf, rr, tile, chunk, h0, rh):
        bk = acc_bank.next()
        for k in range(16):
            P.add("tensor", lambda e, k=k: e.matmul(bank32(bk)[:, 0:256], hT[:, k, tile * 128:(tile + 1) * 128],
                                                    buf[:, k, :], start=(k == 0), stop=(k == 15)),
                  reads=rr + rh, writes=[r_bank[bk]])
        vst, rv = vst_p.next()
        P.add("scalar", lambda e: e.activation(out=vst[:], in_=bank32(bk)[:, 0:256], func=AF.Identity),
              reads=[r_bank[bk]], writes=[rv])
        P.add("sync", lambda e: e.dma_start(out=Vs[h0:h0 + 2, :, chunk, :].rearrange("h p e -> p h e"),
                                             in_=vst[:].rearrange("p (h e) -> p h e", h=2)), reads=[rv], dma=True)

    def glu_store(bk_a, bk_g, c, ncol, dst_ap, mask_ap=None):
        sg, rsg = sg_p.next()
        P.add("scalar", lambda e: e.activation(out=sg[:, 0:ncol], in_=bank32(bk_g)[:, 0:ncol], func=AF.Sigmoid,
                                               bias=bglu[:, 8 + c:9 + c], scale=1.0),
              reads=[r_bank[bk_g], r_cst], writes=[rsg])
        if mask_ap is None:
            P.add("vector", lambda e: e.scalar_tensor_tensor(out=dst_ap, in0=bank32(bk_a)[:, 0:ncol],
                                                             scalar=bglu[:, c:c + 1], in1=sg[:, 0:ncol],
                                                             op0=ALU.add, op1=ALU.mult),
                  reads=[r_bank[bk_a], rsg, r_cst], writes=[r_gT])
        else:
            P.add("vector", lambda e: e.scalar_tensor_tensor(out=sg[:, 0:ncol], in0=bank32(bk_a)[:, 0:ncol],
                                                             scalar=bglu[:, c:c + 1], in1=sg[:, 0:ncol],
                                                             op0=ALU.add, op1=ALU.mult),
                  reads=[r_bank[bk_a], r_cst], writes=[rsg])
            for hh in range(2):
                P.add("vector", lambda e, hh=hh: e.tensor_scalar(
                    out=dst_ap[hh], in0=sg[:, hh * 16:(hh + 1) * 16], scalar1=mask_ap[hh], scalar2=None, op0=ALU.mult),
                    reads=[rsg, r_cst], writes=[r_gT])

    xo_v = x_own.rearrange("(t p) d -> t p d", p=128)
    ln_tiles([(xo_v[t], 0, t * 128) for t in range(16)])
    P.add("sync", lambda e: e.dma_start(out=ropeb[:], in_=rope[0:2].rearrange("a p t -> p a t")), writes=[r_rope], dma=True)
    P.add("vector", lambda e: e.memset(gT[:].rearrange("p a b -> p (a b)"), 0.0), writes=[r_gT])

    rh_own = r_hT[0:4]

    def a2_fn(tag, buf, rr):
        kind, j = tag
        if kind in ("q", "k"):
            for cc in range(2):
                h = 2 * j + cc
                for tt in range(4):
                    bk = proj_fm(buf, rr, cc, tt * 512, 512, [r_hT[tt]])
                    dst = (Qs if kind == "q" else Ks)[h, :, tt * 512:(tt + 1) * 512]
                    rope_defer(bk, r_bank[bk], tt, dst)
            rope_flush()
        elif kind == "v":
            for tile in range(16):
                v_tm(buf, rr, tile, tile, 2 * j, [r_hT[tile // 4]])
        else:
            for tt in range(4):
                bk_a = proj_fm(buf, rr, 0, tt * 512, 512, [r_hT[tt]])
                bk_g = proj_fm(buf, rr, 1, tt * 512, 512, [r_hT[tt]])
                glu_store(bk_a, bk_g, j, 512, gT[:, j, 16 + tt * 512:16 + (tt + 1) * 512])

    specs = [(("q", j), (win_v, 0, j * 256)) for j in range(4)]
    specs += [(("k", j), (win_v, 0, 1024 + j * 256)) for j in range(4)]
    specs += [(("v", j), (win_v, 0, 2048 + j * 256)) for j in range(4)]
    specs += [(("u", c), (win_v, 0, 0, (3072 + c * 128, 4096 + c * 128))) for c in range(8)]
    merged = []
    for sp in specs:
        merged.append(sp)
        for _ in range(2):
            if ada_late:
                merged.append(ada_late.pop(0))
    merged += ada_late

    def a2_dispatch(tag, buf, rr):
        if tag[0] in ("p", "g"):
            ada_fn(tag, buf, rr)
        else:
            a2_fn(tag, buf, rr)
    stream(merged, a2_dispatch)

    xt_v = x_oth.rearrange("(t p) d -> t p d", p=128)
    cx_v = ctxb.rearrange("(t p) d -> t p d", p=128)
    ln_tiles([(xt_v[t], 0, t * 128) for t in range(16)] + [(cx_v[t], 1, 2048 + t * 128) for t in range(2)])
    P.add("sync", lambda e: e.dma_start(out=ropeb[:], in_=rope[2:4].rearrange("a p t -> p a t")), writes=[r_rope], dma=True)

    def a4_fn(tag, buf, rr):
        kind, j = tag
        if kind == "k":
            for cc in range(2):
                h = 2 * j + cc
                for tt in range(4):
                    bk = proj_fm(buf, rr, cc, tt * 512, 512, [r_hT[tt]])
                    rope_defer(bk, r_bank[bk], tt, Ks[h, :, 2048 + tt * 512:2048 + (tt + 1) * 512])
                bk = proj_fm(buf, rr, cc, 2048, 256, [r_hT[4]])
                rope_flush()
                qr, rqr = qr_p.next()
                P.add("scalar", lambda e, bk=bk, qr=qr: e.activation(out=qr[:, 0:256], in_=bank32(bk)[:, 0:256], func=AF.Identity),
                      reads=[r_bank[bk]], writes=[rqr])
                P.add("sync", lambda e, h=h, qr=qr: e.dma_start(out=Ks[h, :, 4096:4352], in_=qr[:, 0:256]), reads=[rqr], dma=True)
        elif kind == "v":
            for tile in range(18):
                v_tm(buf, rr, tile, 16 + tile, 2 * j, [r_hT[tile // 4]])
        else:
            bk_a = acc_bank.next()
            bk_g = acc_bank.next()
            for (bk, cc) in ((bk_a, 0), (bk_g, 1)):
                for hh, t0 in enumerate((0, 2032)):
                    for k in range(16):
                        P.add("tensor", lambda e, k=k, bk=bk, cc=cc, hh=hh, t0=t0: e.matmul(
                            bank32(bk)[:, hh * 16:(hh + 1) * 16], buf[:, k, cc * 128:(cc + 1) * 128],
                            hT[:, k, t0:t0 + 16], start=(k == 0), stop=(k == 15)),
                            reads=rr + [r_hT[0], r_hT[3]], writes=[r_bank[bk]])
            glu_store(bk_a, bk_g, j, 32, [gT[:, j, 2064:2080], gT[:, j, 0:16]], mask_ap=[hm[:, 1:2], hm[:, 0:1]])

    specs = [(("k", j), (win_v, 0, 1024 + j * 256)) for j in range(4)]
    specs += [(("v", j), (win_v, 0, 2048 + j * 256)) for j in range(4)]
    specs += [(("u", c), (win_v, 0, 0, (3072 + c * 128, 4096 + c * 128))) for c in range(8)]
    stream(specs, a4_fn)
    P.barrier()

    a5_mark = A.off
    A.off = ph_mark
    diag_all = A.alloc("diag", [128, 8, 31, 128], BF16)
    r_diag = [Res("diag%d" % i) for i in range(8)]
    c32 = A.alloc("c32", [128, 8, 512], F32)
    r_c32b = Res("c32b")
    csq_p = Pool([(A.alloc("csq", [128, 512], F32), Res("csq%d" % i)) for i in range(2)])
    meanb = A.alloc("meanb", [128, 512], F32)
    msq = A.alloc("msq", [128, 512], F32)
    rstdb = A.alloc("rstdb", [128, 512], F32)
    r_cstat = Res("cstat")
    ct_p = Pool([(A.alloc("ct", [128, 512], F32), Res("ct%d" % i)) for i in range(2)])
    co_p = Pool([(A.alloc("co", [128, 512], BF16), Res("co%d" % i)) for i in range(2)])
    assert A.off <= gT_start
    for c in range(8):
        for j in range(31):
            P.add("vector", lambda e, c=c, j=j: e.tensor_scalar(
                out=diag_all[:, c, j, :], in0=ident[:], scalar1=wdw[:, c * 31 + j:c * 31 + j + 1], scalar2=None,
                op0=ALU.mult), reads=[r_cst], writes=[r_diag[c]], loose=True)
    di = [0]
    pend_stats = []
    for tt in range(4):
        for c in range(8):
            bk = c % 2
            for j in range(31):
                P.add("tensor", lambda e, c=c, j=j, bk=bk, tt=tt: e.matmul(
                    bank32(bk), diag_all[:, c, j, :], gT[:, c, tt * 512 + j + 1:tt * 512 + j + 513],
                    start=(j == 0), stop=(j == 30)), reads=[r_diag[c], r_gT], writes=[r_bank[bk]])
            csq, rcsq = csq_p.next()
            P.add("scalar", lambda e, c=c, bk=bk: e.activation(out=c32[:, c, :], in_=bank32(bk), func=AF.Identity,
                                                               bias=cv[:, c:c + 1], scale=1.0),
                  reads=[r_bank[bk], r_cst], writes=[r_c32b], loose=True)
            P.add("scalar", lambda e, c=c, bk=bk, csq=csq: e.activation(out=csq[:], in_=bank32(bk), func=AF.Square,
                                                                       bias=cv[:, c:c + 1], scale=1.0),
                  reads=[r_bank[bk], r_cst], writes=[rcsq])
            def stats_mm(c=c, csq=csq, rcsq=rcsq):
                P.add("tensor", lambda e: e.matmul(bank32(4), ones32[:], c32[:, c, :], start=(c == 0), stop=(c == 7)),
                      reads=[r_c32b, r_cst], writes=[r_bank[4]])
                P.add("tensor", lambda e: e.matmul(bank32(5), ones32[:], csq[:], start=(c == 0), stop=(c == 7)),
                      reads=[rcsq, r_cst], writes=[r_bank[5]])
            pend_stats.append(stats_mm)
            if len(pend_stats) > 1:
                pend_stats.pop(0)()
        while pend_stats:
            pend_stats.pop(0)()
        P.add("scalar", lambda e: e.activation(out=meanb[:], in_=bank32(4), func=AF.Identity, scale=1.0 / 1024),
              reads=[r_bank[4]], writes=[r_cstat])
        P.add("vector", lambda e: e.tensor_tensor(out=msq[:], in0=meanb[:], in1=meanb[:], op=ALU.mult), writes=[r_cstat])
        P.add("vector", lambda e: e.scalar_tensor_tensor(out=msq[:], in0=bank32(5), scalar=1.0 / 1024, in1=msq[:],
                                                         op0=ALU.mult, op1=ALU.subtract), reads=[r_bank[5]], writes=[r_cstat])
        P.add("scalar", lambda e: e.activation(out=msq[:], in_=msq[:], func=AF.Sqrt, bias=EPS, scale=1.0), writes=[r_cstat])
        P.add("vector", lambda e: e.reciprocal(out=rstdb[:], in_=msq[:]), writes=[r_cstat])
        for c in range(8):
            ct, rct = ct_p.next()
            co, rco = co_p.next()
            P.add("vector", lambda e, c=c, ct=ct: e.tensor_tensor(out=ct[:], in0=c32[:, c, :], in1=meanb[:], op=ALU.subtract),
                  reads=[r_c32b, r_cstat], writes=[rct])
            P.add("vector", lambda e, ct=ct: e.tensor_tensor(out=ct[:], in0=ct[:], in1=rstdb[:], op=ALU.mult),
                  reads=[r_cstat], writes=[rct])
            P.add("scalar", lambda e, c=c, ct=ct, co=co: e.activation(out=co[:], in_=ct[:], func=AF.Silu,
                                                                      scale=cv[:, 8 + c:9 + c], bias=cv[:, 16 + c:17 + c]),
                  reads=[rct, r_cst], writes=[rco])
            P.add("sync", lambda e, c=c, co=co, tt=tt: e.dma_start(out=ACs[8 + c, :, tt * 512:(tt + 1) * 512], in_=co[:]),
                  reads=[rco], dma=True)
    P.barrier()
    A.off = ph_mark
    if stop_after == "A":
        return _finish(nc, P, out, A)

    Kh = [A.alloc("Kh", [128, NKEY], BF16) for _ in range(2)]
    Vh = [A.alloc("Vh", [128, 34, 130], BF16) for _ in range(2)]
    Qh = [A.alloc("Qh", [128, T], BF16) for _ in range(2)]
    r_k = [Res("k0"), Res("k1")]
    r_q = [Res("q0"), Res("q1")]
    r_v = [Res("v0"), Res("v1")]
    Pt_p = Pool([(A.alloc("Pt", [128, 1024], BF16), Res("Pt%d" % i)) for i in range(3)])
    attn = A.alloc("attn", [128, 16, 1024], BF16)
    r_attn = Res("attn")
    ob_p = Pool([(A.alloc("ob", [128, 128], F32), Res("ob%d" % i)) for i in range(2)])
    junk = A.alloc("junk", [128, 128], F32)
    rsb = A.alloc("rsb", [128, 16], F32)
    r_rs = Res("rs")
    aT_p = Pool([(A.alloc("aT", [128, 8, 128], BF16), Res("aT%d" % i)) for i in range(2)])
    for b in range(2):
        P.add("vector", lambda e, b=b: e.memset(Vh[b][:].rearrange("p a b -> p (a b)"), 1.0), writes=[r_v[b]])

    def load_head(h):
        b = h % 2
        P.add("sync", lambda e: e.dma_start(out=Kh[b][:], in_=Ks[h]), writes=[r_k[b]], dma=True)
        P.add("sync", lambda e: e.dma_start(out=Qh[b][:], in_=Qs[h]), writes=[r_q[b]], dma=True)
        P.add("sync", lambda e: e.dma_start(out=Vh[b][:, :, 0:128], in_=Vs[h]), writes=[r_v[b]], dma=True)

    SC = 0.125
    def ogrp(g):
        bk = 4 + g // 3
        off = (g % 3) * 160
        return bk, bank32(bk)[:, off:off + 129]

    oc_p = Pool([(A.alloc("oc", [128, 3, 512], F32), Res("oc%d" % i)) for i in range(2)])
    ob4 = [A.alloc("ob4", [128, 128], F32) for _ in range(4)]
    r_ob4 = [Res("ob4_%d" % i) for i in range(4)]
    iters = [(h, qb, kc) for h in range(8) for qb in range(4) for kc in range(34)]

    def emit_qk(i):
        h, qb, kc = iters[i]
        b = h % 2
        sb = 2 * (i % 2)
        K_, Q_ = Kh[b], Qh[b]
        for c in range(2):
            P.add("tensor", lambda e, c=c: e.matmul(
                bank32(sb + c), K_[c * 64:(c + 1) * 64, kc * 128:(kc + 1) * 128],
                Q_[c * 64:(c + 1) * 64, qb * 512:(qb + 1) * 512], start=True, stop=True),
                reads=[r_k[b], r_q[b]], writes=[r_bank[sb + c]])

    def emit_exp_pv(i):
        h, qb, kc = iters[i]
        b = h % 2
        sb = 2 * (i % 2)
        V_ = Vh[b]
        Pt, rPt = Pt_p.next()
        P.add("scalar", lambda e: e.activation(
            out=Pt[:], in_=psm[:, sb:sb + 2, :].rearrange("p a b -> p (a b)"), func=AF.Exp, scale=SC),
            reads=[r_bank[sb], r_bank[sb + 1]], writes=[rPt])
        for g in range(8):
            c, qs = g // 4, g % 4
            bk, oap = ogrp(g)
            P.add("tensor", lambda e, g=g, c=c, qs=qs, oap=oap: e.matmul(
                oap, Pt[:, c * 512 + qs * 128:c * 512 + (qs + 1) * 128], V_[:, kc, 0:129],
                start=(kc == 0 and g % 3 == 0), stop=(kc == 33), skip_group_check=True),
                reads=[rPt, r_v[b]], writes=[r_bank[bk]])

    def emit_evac(h, qb):
        oc, roc = oc_p.next()
        for g in range(8):
            j, o_ = g // 3, (g % 3) * 160
            P.add("vector", lambda e, j=j, o_=o_: e.tensor_copy(out=oc[:, j, o_:o_ + 129], in_=bank32(4 + j)[:, o_:o_ + 129]),
                  reads=[r_bank[4 + j]], writes=[roc], loose=(g > 0))
        for j in range(3):
            ng = 3 if j < 2 else 2
            P.add("vector", lambda e, j=j, ng=ng: e.reciprocal(
                out=rsb[:, 8 + 3 * j:8 + 3 * j + ng],
                in_=oc[:, j, 0:480].rearrange("p (g c) -> p g c", c=160)[:, 0:ng, 128:129]),
                reads=[roc], writes=[r_rs])
        P.add("vector", lambda e: e.tensor_scalar(out=rsb[:, 12:16], in0=rsb[:, 12:16], scalar1=lamv[:, 2:3], scalar2=None,
                                                  op0=ALU.mult), reads=[r_lam], writes=[r_rs])

        def og(g):
            return oc[:, g // 3, (g % 3) * 160:(g % 3) * 160 + 128]
        for qs in range(4):
            ob = ob4[qs]
            P.add("vector", lambda e, ob=ob, qs=qs: e.tensor_scalar(
                out=ob[:], in0=og(qs), scalar1=rsb[:, 8 + qs:9 + qs], scalar2=None, op0=ALU.mult),
                reads=[roc, r_rs], writes=[r_ob4[qs]])
            P.add("vector", lambda e, ob=ob, qs=qs: e.scalar_tensor_tensor(
                out=ob[:], in0=og(4 + qs), scalar=rsb[:, 12 + qs:13 + qs], in1=ob[:], op0=ALU.mult, op1=ALU.add),
                reads=[roc, r_rs], writes=[r_ob4[qs]])
            P.add("vector", lambda e, ob=ob, qs=qs: e.scalar_tensor_tensor(
                out=junk[:], in0=ob[:], scalar=1.0, in1=ob[:], op0=ALU.mult, op1=ALU.mult,
                accum_out=rsb[:, qs:qs + 1]), reads=[r_ob4[qs]], writes=[r_rs])
        def tail():
            P.add("scalar", lambda e: e.activation(out=rsb[:, 0:4], in_=rsb[:, 0:4], func=AF.Sqrt, bias=EPS, scale=1.0 / 128),
                  writes=[r_rs])
            P.add("vector", lambda e: e.reciprocal(out=rsb[:, 4:8], in_=rsb[:, 0:4]), writes=[r_rs])
            for qs in range(4):
                ob = ob4[qs]
                qt = qb * 4 + qs
                P.add("vector", lambda e, ob=ob, qs=qs, qt=qt: e.scalar_tensor_tensor(
                    out=attn[:, qt, h * 128:(h + 1) * 128], in0=ob[:], scalar=rsb[:, 4 + qs:5 + qs], in1=gsub[:],
                    op0=ALU.mult, op1=ALU.mult), reads=[r_ob4[qs], r_rs, r_cst], writes=[r_attn])
        return tail

    _save_off = A.off
    A.off = A.limit - (2 * D * 4 + 4 * D * 2) - 64
    lng = [A.alloc("ln1g", [128, D], F32), A.alloc("ln2g", [128, D], F32)]
    lnb = [A.alloc("ln1b", [128, D], BF16), A.alloc("ln2b", [128, D], BF16)]
    brow = [A.alloc("bo_row", [128, D], BF16), A.alloc("b2_row", [128, D], BF16)]
    top_reserved = A.limit - (2 * D * 4 + 4 * D * 2) - 64
    A.off = _save_off
    assert A.off <= top_reserved
    r_ln = Res("ln")
    for i, src in enumerate((ln1, ln2)):
        P.add("sync", lambda e, i=i, src=src: e.dma_start(out=lng[i][:], in_=src[0].partition_broadcast(128)), writes=[r_ln], dma=True)
        P.add("gpsimd", lambda e, i=i, src=src: e.dma_start(out=lnb[i][:, 0:1024], in_=src[1, 0:1024].partition_broadcast(128)),
              writes=[r_ln], dma=True)
        P.add("gpsimd", lambda e, i=i, src=src: e.dma_start(out=lnb[i][:, 1024:2048], in_=src[1, 1024:2048].partition_broadcast(128)),
              writes=[r_ln], dma=True)
    for i, src in enumerate((b_out, b_ff2)):
        P.add("vector", lambda e, i=i: e.memset(brow[i][:], 0.0), writes=[r_ln])
        for hh in range(2):
            P.add("gpsimd", lambda e, i=i, src=src, hh=hh: e.dma_start(out=brow[i][0:1, hh * 1024:(hh + 1) * 1024],
                                                                      in_=src[0:1, hh * 1024:(hh + 1) * 1024]),
                  writes=[r_ln], dma=True)
    pend_tail = []
    load_head(0)
    load_head(1)
    emit_qk(0)
    for i, (h, qb, kc) in enumerate(iters):
        if i + 1 < len(iters):
            emit_qk(i + 1)
        emit_exp_pv(i)
        if kc == 3 and pend_tail:
            pend_tail.pop(0)()
        if kc == 33:
            pend_tail.append(emit_evac(h, qb))
            if qb == 3 and h + 2 < 8:
                load_head(h + 2)
    while pend_tail:
        pend_tail.pop(0)()
    for qt in range(16):
        bk = qt % 2
        aT, raT = aT_p.next()
        for k in range(8):
            P.add("tensor", lambda e, k=k, qt=qt, bk=bk: e.transpose(
                bank16(bk)[:, k * 128:(k + 1) * 128], attn[:, qt, k * 128:(k + 1) * 128], ident[:]),
                reads=[r_attn, r_cst], writes=[r_bank[bk]])
        P.add("vector", lambda e, aT=aT, bk=bk: e.tensor_copy(out=aT[:].rearrange("p a b -> p (a b)"), in_=bank16(bk)),
              reads=[r_bank[bk]], writes=[raT])
        P.add("sync", lambda e, aT=aT, qt=qt: e.dma_start(out=ACs[0:8, :, qt * 128:(qt + 1) * 128].rearrange("k p t -> p k t"),
                                                          in_=aT[:]), reads=[raT], dma=True)
    P.barrier()
    A.off = ph_mark
    if stop_after == "B":
        return _finish(nc, P, out, A)

    act = A.alloc("act", [128, 16, 512], BF16)
    r_act = Res("act")
    x1 = A.alloc("x1", [128, 4, D], F32)
    r_x1 = [Res("x1_%d" % i) for i in range(4)]
    hid = A.alloc("hid", [128, 64, 512], BF16)
    r_hid = Res("hid")
    nrmCs = [A.alloc("nrmc", [128, D], BF16) for _ in range(2)]
    r_nrmCs = [Res("nrmc0"), Res("nrmc1")]
    statCs = [(A.alloc("sttc", [128, 4, 6], F32), A.alloc("mvc", [128, 8], F32), Res("statc%d" % i)) for i in range(3)]
    statT = [[(A.alloc("sttA", [128, 4, 6], F32), A.alloc("mvA", [128, 8], F32), Res("stT%d_%d" % (t_, u_)))
              for u_ in range(2)] for t_ in range(4)]
    stat_i = [0]

    def next_stat():
        stat_i[0] += 1
        return statCs[stat_i[0] % 3]

    tmp_p = Pool([(A.alloc("tmpc", [128, 256], F32), Res("tmpc%d" % i)) for i in range(2)])
    rl_p = Pool([(A.alloc("rl", [128, 512], F32), Res("rl%d" % i)) for i in range(2)])
    assert A.off <= top_reserved, (A.off, top_reserved)
    wout_v = wview(w_out)
    w1_v = wview(w_ff1)
    w2_v = wview(w_ff2)
    out_v = out.rearrange("(t p) d -> t p d", p=128)
    acc6 = Pool([0, 1, 2, 3, 6, 7])

    def ln_apply(src_tile_fn, gi, dst_tile_fn, rsrc, rdst, rdst_halves=None):
        stC = next_stat()
        mvC = stC[1]
        ln_stats(lambda c4: src_tile_fn()[:, c4 * 512:(c4 + 1) * 512], stC, rsrc)
        P.add("scalar", lambda e: e.activation(out=dst_tile_fn(), in_=src_tile_fn(), func=AF.Identity,
                                               scale=mvC[:, 3:4], bias=mvC[:, 4:5]),
              reads=[rsrc, stC[2]], writes=[rdst])
        P.add("vector", lambda e: e.tensor_tensor(out=dst_tile_fn(), in0=dst_tile_fn(), in1=lng[gi][:], op=ALU.mult),
              reads=[r_ln], writes=[rdst])
        P.add("vector", lambda e: e.tensor_tensor(out=dst_tile_fn(), in0=dst_tile_fn(), in1=lnb[gi][:], op=ALU.add),
              reads=[r_ln], writes=[rdst])

    def load_act(tb):
        P.add("sync", lambda e: e.dma_start(out=act[:], in_=ACs[:, :, tb * 512:(tb + 1) * 512].rearrange("k p t -> p k t")),
              writes=[r_act], dma=True)

    load_act(0)
    for tb in range(4):
        for tt in range(4):
            P.add("sync", lambda e, tb=tb, tt=tt: e.dma_start(out=x1[:, tt, :], in_=xo_v[tb * 4 + tt]), writes=[r_x1[tt]], dma=True)

        def op_fn(tag, buf, rr, tb=tb):
            cb = tag
            for tt in range(4):
                bk = acc_bank.next()
                P.add("tensor", lambda e, bk=bk: e.matmul(bank32(bk)[:, 0:256], ones_bf[:], brow[0][:, cb * 256:(cb + 1) * 256],
                                                          start=True, stop=False), reads=[r_ln, r_cst], writes=[r_bank[bk]])
                for k in range(16):
                    P.add("tensor", lambda e, k=k, bk=bk, tt=tt: e.matmul(
                        bank32(bk)[:, 0:256], act[:, k, tt * 128:(tt + 1) * 128], buf[:, k, :], start=False, stop=(k == 15)),
                        reads=rr + [r_act], writes=[r_bank[bk]])
                tmp, rtmp = tmp_p.next()
                P.add("vector", lambda e, bk=bk, tmp=tmp: e.tensor_tensor(
                    out=tmp[:], in0=bank32(bk)[:, 0:256], in1=g_bc[0][:, cb * 256:(cb + 1) * 256],
                    op=ALU.mult), reads=[r_bank[bk], r_g[0]], writes=[rtmp])
                P.add("vector", lambda e, tt=tt, tmp=tmp: e.scalar_tensor_tensor(
                    out=x1[:, tt, cb * 256:(cb + 1) * 256], in0=x1[:, tt, cb * 256:(cb + 1) * 256], scalar=ALPHA,
                    in1=tmp[:], op0=ALU.mult, op1=ALU.add), reads=[rtmp], writes=[r_x1[tt]])
        stream([(cb, (wout_v, 0, cb * 256)) for cb in range(8)], op_fn)

        def c_stage(sg, tt):
            xt = x1[:, tt, :]
            stA, stB = statT[tt]
            nrmC, r_nrmC = nrmCs[tt % 2], r_nrmCs[tt % 2]

            def stats_(st):
                stt_, mv_, rst_ = st
                for c4 in range(4):
                    P.add("vector", lambda e, c4=c4: e.bn_stats(out=stt_[:, c4, :], in_=x1[:, tt, c4 * 512:(c4 + 1) * 512]),
                          reads=[r_x1[tt]], writes=[rst_], loose=(c4 > 0))
                P.add("vector", lambda e: e.bn_aggr(out=mv_[:, 0:2], in_=stt_[:].rearrange("p a b -> p (a b)")), writes=[rst_])
                P.add("scalar", lambda e: e.activation(out=mv_[:, 2:3], in_=mv_[:, 1:2], func=AF.Sqrt, bias=EPS, scale=1.0),
                      writes=[rst_])

            def rstd_(st):
                stt_, mv_, rst_ = st
                P.add("vector", lambda e: e.reciprocal(out=mv_[:, 3:4], in_=mv_[:, 2:3]), writes=[rst_])
                P.add("vector", lambda e: e.tensor_scalar(out=mv_[:, 4:5], in0=mv_[:, 0:1], scalar1=mv_[:, 3:4],
                                                          scalar2=-1.0, op0=ALU.mult, op1=ALU.mult), writes=[rst_])
            if sg == 0:
                stats_(stA)
            elif sg == 1:
                rstd_(stA)
                P.add("scalar", lambda e: e.activation(out=xt, in_=xt, func=AF.Identity, scale=stA[1][:, 3:4], bias=stA[1][:, 4:5]),
                      reads=[stA[2]], writes=[r_x1[tt]])
            elif sg == 2:
                P.add("vector", lambda e: e.tensor_tensor(out=xt, in0=xt, in1=lng[0][:], op=ALU.mult), reads=[r_ln], writes=[r_x1[tt]])
                P.add("vector", lambda e: e.tensor_tensor(out=xt, in0=xt, in1=lnb[0][:], op=ALU.add), reads=[r_ln], writes=[r_x1[tt]])
            elif sg == 3:
                stats_(stB)
            elif sg == 4:
                rstd_(stB)
                P.add("scalar", lambda e: e.activation(out=nrmC[:], in_=xt, func=AF.Identity, scale=stB[1][:, 3:4], bias=stB[1][:, 4:5]),
                      reads=[r_x1[tt], stB[2]], writes=[r_nrmC])
            else:
                transpose_mod(nrmC, r_nrmC, 4 if tt % 2 == 0 else 6, 48, 64, 0,
                              lambda k: act[:, k, tt * 128:(tt + 1) * 128], r_act)
        for step in range(4 + 5):
            for tt in range(4):
                sg = step - tt
                if 0 <= sg <= 5:
                    c_stage(sg, tt)

        def f1_fn(tag, buf, rr):
            jb = tag
            for cc in range(2):
                ch = jb * 2 + cc
                bk = acc6.next()
                for k in range(16):
                    P.add("tensor", lambda e, k=k, bk=bk, cc=cc: e.matmul(
                        bank32(bk), buf[:, k, cc * 128:(cc + 1) * 128], act[:, k, :], start=(k == 0), stop=(k == 15)),
                        reads=rr + [r_act], writes=[r_bank[bk]])
                rl, rrl = rl_p.next()
                P.add("vector", lambda e, bk=bk, ch=ch, rl=rl: e.tensor_scalar(
                    out=rl[:], in0=bank32(bk), scalar1=b1s[:, ch:ch + 1], scalar2=0.0, op0=ALU.add, op1=ALU.max),
                    reads=[r_bank[bk], r_cst], writes=[rrl])
                P.add("scalar", lambda e, ch=ch, rl=rl: e.activation(out=hid[:, ch, :], in_=rl[:], func=AF.Square),
                      reads=[rrl], writes=[r_hid], loose=True)
        stream([(jb, (w1_v, 0, jb * 256)) for jb in range(32)], f1_fn)
        if tb + 1 < 4:
            load_act(tb + 1)

        def f2_fn(tag, buf, rr):
            cb, jg = tag
            for tt in range(4):
                bk = tt
                if jg == 0:
                    P.add("tensor", lambda e, bk=bk: e.matmul(bank32(bk)[:, 0:256], ones_bf[:], brow[1][:, cb * 256:(cb + 1) * 256],
                                                              start=True, stop=False), reads=[r_ln, r_cst], writes=[r_bank[bk]])
                for j in range(16):
                    P.add("tensor", lambda e, j=j, bk=bk, tt=tt: e.matmul(
                        bank32(bk)[:, 0:256], hid[:, jg * 16 + j, tt * 128:(tt + 1) * 128], buf[:, j, :],
                        start=False, stop=(jg == 3 and j == 15)), reads=rr + [r_hid], writes=[r_bank[bk]])
                if jg == 3:
                    tmp, rtmp = tmp_p.next()
                    P.add("vector", lambda e, bk=bk, tmp=tmp: e.tensor_tensor(
                        out=tmp[:], in0=bank32(bk)[:, 0:256], in1=g_bc[1][:, cb * 256:(cb + 1) * 256], op=ALU.mult),
                        reads=[r_bank[bk], r_g[1]], writes=[rtmp])
                    P.add("vector", lambda e, tt=tt, tmp=tmp: e.scalar_tensor_tensor(
                        out=x1[:, tt, cb * 256:(cb + 1) * 256], in0=x1[:, tt, cb * 256:(cb + 1) * 256], scalar=ALPHA,
                        in1=tmp[:], op0=ALU.mult, op1=ALU.add), reads=[rtmp], writes=[r_x1[tt]])
        stream([((cb, jg), (w2_v, jg * 16, cb * 256)) for cb in range(8) for jg in range(4)], f2_fn)

        for tt in range(4):
            ln_apply(lambda tt=tt: x1[:, tt, :], 1, lambda tt=tt: x1[:, tt, :], r_x1[tt], r_x1[tt])
            P.add("sync", lambda e, tb=tb, tt=tt: e.dma_start(out=out_v[tb * 4 + tt], in_=x1[:, tt, :]), reads=[r_x1[tt]], dma=True)
    return _finish(nc, P, out, A)


def _finish(nc, P, out, A):
    P.emit()
    return nc, P


def _rope_tables(pos):
    row = (pos // 64).astype(np.float32)
    col = (pos % 64).astype(np.float32)
    inv = (10000.0 ** (-np.arange(0, 32, 2, dtype=np.float32) / 32.0)).astype(np.float32)
    ar = row[:, None] * inv[None, :]
    ac = col[:, None] * inv[None, :]
    ang = np.concatenate([ar, ar, ac, ac], axis=-1)
    cos = np.cos(ang).astype(np.float32)
    sin = np.sin(ang).astype(np.float32)
    sgn = np.where((np.arange(64) % 32) < 16, -1.0, 1.0).astype(np.float32)
    sin = sin * sgn[None, :]
    cosT = np.concatenate([cos.T, cos.T], axis=0)
    sinT = np.concatenate([sin.T, sin.T], axis=0)
    return np.ascontiguousarray(cosT), np.ascontiguousarray(sinT)


def _consts():
    c = np.zeros((128, 384), np.float32)
    c[:, 0:128] = np.eye(128, dtype=np.float32)
    pm = np.zeros((128, 128), np.float32)
    for m in range(128):
        d = m % 64
        pd = d + 16 if (d % 32) < 16 else d - 16
        pm[(m // 64) * 64 + pd, m] = 1.0
    c[:, 128:256] = pm
    c[:, 256:384] = 1.0
    return c


def make_in_maps(inp):
    f = lambda a: np.ascontiguousarray(np.asarray(a, dtype=np.float32))
    x = f(inp["x"]); c = f(inp["c"]); ctx = f(inp["ctx"]); c_ctx = f(inp["c_ctx"])
    w_ada = f(inp["w_ada"][0]); b_ada = f(inp["b_ada"][0]); w_in = f(inp["w_in"][0])
    b_glu = f(inp["b_glu"][0])
    shared = {
        "w_ada": w_ada, "w_in": w_in, "w_out": f(inp["w_out"][0]), "w_ff1": f(inp["w_ff1"][0]),
        "w_ff2": f(inp["w_ff2"][0]),
        "b_adaT": f(b_ada.reshape(96, 128).T),
        "b_ada_g": f(np.stack([b_ada[2 * D:3 * D], b_ada[5 * D:6 * D]])),
        "b_gluT": f(b_glu.reshape(16, 128).T),
        "lam4": f(np.stack([inp["lambda_q1"][0], inp["lambda_k1"][0], inp["lambda_q2"][0], inp["lambda_k2"][0]])),
        "subg": f(inp["subln_g"][0].reshape(1, 128)),
        "w_dwT": f(np.asarray(inp["w_dw"][0]).reshape(31, 8, 128).transpose(2, 1, 0).reshape(128, 8 * 31)),
        "cvec": f(np.concatenate([np.asarray(inp[k][0]).reshape(8, 128).T for k in ("b_dw", "conv_ln_g", "conv_ln_b")], axis=1)),
        "b_out": f(inp["b_out"][0].reshape(1, D)),
        "ln1": f(np.stack([inp["ln1_g"][0], inp["ln1_b"][0]])),
        "b1T": f(np.asarray(inp["b_ff1"][0]).reshape(64, 128).T),
        "b_ff2": f(inp["b_ff2"][0].reshape(1, D)),
        "ln2": f(np.stack([inp["ln2_g"][0], inp["ln2_b"][0]])),
        "consts": _consts(),
    }
    maps = []
    for core in range(8):
        b, s = core // 2, core % 2
        own = np.arange(s * T, (s + 1) * T)
        oth = np.arange((1 - s) * T, (2 - s) * T)
        co, so = _rope_tables(own)
        ct, st_ = _rope_tables(oth)
        hm = np.zeros((128, 2), np.float32)
        hm[:, 0] = 1.0 if s == 1 else 0.0
        hm[:, 1] = 1.0 if s == 0 else 0.0
        m = dict(shared)
        m.update({
            "x_own": f(x[b, own]), "x_oth": f(x[b, oth]), "ctxb": f(ctx[b]),
            "cT2": f(np.stack([c[b].reshape(16, 128).T, c_ctx.reshape(16, 128).T], axis=-1).reshape(128, 32)),
            "rope": f(np.stack([co, so, ct, st_])),
            "hmask": hm,
        })
        maps.append(m)
    return maps


_CACHE = {}


def kernel(**inputs):
    if "nc" not in _CACHE:
        _CACHE["nc"] = build_program()[0]
    nc = _CACHE["nc"]
    maps = make_in_maps(inputs)
    res = run_bass_kernel_spmd(nc, maps, core_ids=list(range(8)))
    outp = np.empty((4, L, D), np.float32)
    for core in range(8):
        b, s = core // 2, core % 2
        outp[b, s * T:(s + 1) * T] = np.asarray(res.results[core]["out"], dtype=np.float32)
    return outp
```

```python
import contextlib
import math
import numpy as np
import concourse.bass as bass
import concourse.mybir as mybir
from concourse.bass_utils import run_bass_kernel_spmd

F32 = mybir.dt.float32
BF16 = mybir.dt.bfloat16
AF = mybir.ActivationFunctionType
ALU = mybir.AluOpType
ENGS = ["sync", "scalar", "vector", "gpsimd", "tensor"]

D = 2048
L = 4096
T = 2048
CTX = 256
NKEY = 4352
EPS = 1e-5
ALPHA = 2.0 ** 0.25
LAM_INIT = 0.2


class Res:
    __slots__ = ("name", "last_w", "readers")

    def __init__(self, name):
        self.name = name
        self.last_w = None
        self.readers = []


class Op:
    __slots__ = ("eng", "fn", "deps", "dma", "signal", "event", "idx")


class Prog:
    def __init__(self, nc, n_dma_sems=16):
        self.nc = nc
        self.ops = []
        self.n_dma_sems = n_dma_sems
        self.last_op = {e: None for e in ENGS}
        self.pend = {e: [] for e in ENGS}
        self.dmas_since = []

    def add(self, eng, fn, reads=(), writes=(), dma=False, loose=False):
        op = Op()
        op.eng = eng
        op.fn = fn
        op.dma = dma
        op.signal = dma
        op.event = None
        op.idx = len(self.ops)
        deps = list(self.pend[eng])
        self.pend[eng] = []
        for r in reads:
            if r.last_w is not None:
                deps.append(r.last_w)
            if dma:
                r.readers.append(op)
            else:
                r.readers = [o for o in r.readers if o.dma or o.eng != eng]
                r.readers.append(op)
        for w in writes:
            if w.last_w is not None:
                deps.append(w.last_w)
            deps.extend(w.readers)
            w.last_w = op
            w.readers = []
        dd = []
        seen = set()
        for d in deps:
            if d is op or d.idx in seen:
                continue
            if eng == "tensor" and d.eng == "tensor" and not d.dma:
                continue
            if loose and d.eng == eng and not d.dma:
                continue
            seen.add(d.idx)
            dd.append(d)
            d.signal = True
        op.deps = dd
        self.ops.append(op)
        self.last_op[eng] = op
        if dma:
            self.dmas_since.append(op)
        return op

    def barrier(self):
        lst = [o for o in self.last_op.values() if o is not None] + self.dmas_since
        self.dmas_since = []
        for e in ENGS:
            self.pend[e] = list(lst)
        for o in lst:
            o.signal = True

    def emit(self, final_wait_eng="sync"):
        nc = self.nc
        with contextlib.ExitStack() as st:
            engsem = {e: st.enter_context(nc.semaphore("c_" + e)) for e in ENGS}
            dsem = {e: [st.enter_context(nc.semaphore("d_%s_%d" % (e, i))) for i in range(self.n_dma_sems)]
                    for e in ("sync", "scalar", "gpsimd")}
            cnt = {e: 0 for e in ENGS}
            dval = {e: [0] * self.n_dma_sems for e in dsem}
            dlast = {e: [None] * self.n_dma_sems for e in dsem}
            drr = {e: 0 for e in dsem}
            for op in self.ops:
                if op.dma:
                    e = op.eng
                    i = drr[e]
                    drr[e] = (i + 1) % self.n_dma_sems
                    prev = dlast[e][i]
                    if prev is not None:
                        op.deps.append(prev)
                    dval[e][i] += 16
                    op.event = (dsem[e][i], dval[e][i], ("d", e, i))
                    dlast[e][i] = op
                elif op.signal:
                    cnt[op.eng] += 1
                    op.event = (engsem[op.eng], cnt[op.eng], ("c", op.eng))
            tail = [op for op in self.ops if op.dma]
            streams = {e: [o for o in self.ops if o.eng == e] for e in ENGS}
            self.stats = {e: len(streams[e]) for e in ENGS}
            self.cnts = dict(cnt)
            block = st.enter_context(nc.Block())

            def run_stream(ename, eng):
                waited = {}
                for op in streams[ename]:
                    for d in op.deps:
                        sem, val, key = d.event
                        if waited.get(key, 0) < val:
                            eng.wait_ge(sem, val)
                            waited[key] = val
                    ins = op.fn(eng)
                    if op.event is not None:
                        sem, val, key = op.event
                        ins.then_inc(sem, 16 if op.dma else 1)
                if ename == final_wait_eng:
                    mx = {}
                    for op in tail:
                        sem, val, key = op.event
                        if key not in mx or mx[key][1] < val:
                            mx[key] = (sem, val)
                    for key, (sem, val) in mx.items():
                        if waited.get(key, 0) < val:
                            eng.wait_ge(sem, val)
                            waited[key] = val

            @block.sync
            def _(eng):
                run_stream("sync", eng)

            @block.scalar
            def _(eng):
                run_stream("scalar", eng)

            @block.vector
            def _(eng):
                run_stream("vector", eng)

            @block.gpsimd
            def _(eng):
                run_stream("gpsimd", eng)

            @block.tensor
            def _(eng):
                run_stream("tensor", eng)


class Arena:
    def __init__(self, nc, limit):
        self.nc = nc
        self.off = 16512
        self.limit = limit
        self.n = 0

    def alloc(self, name, shape, dt):
        esz = 2 if dt == BF16 else 4
        size = esz * int(np.prod(shape[1:]))
        size = (size + 31) // 32 * 32
        self.n += 1
        t = self.nc.alloc_sbuf_tensor_at("%s_%d" % (name, self.n), list(shape), dt, offset=self.off)
        self.off += size
        assert self.off <= self.limit, (name, self.off, self.limit)
        return t


class Pool:
    def __init__(self, items):
        self.items = items
        self.i = 0

    def next(self):
        it = self.items[self.i % len(self.items)]
        self.i += 1
        return it


def build_program(debug=False, stop_after="D"):
    nc = bass.Bass("TRN2", target_bir_lowering=False)
    P = Prog(nc)

    def din(name, shape):
        return nc.dram_tensor(name, list(shape), F32, kind="ExternalInput").ap()

    x_own = din("x_own", [T, D])
    x_oth = din("x_oth", [T, D])
    ctxb = din("ctxb", [CTX, D])
    cT2 = din("cT2", [128, 32])
    w_ada = din("w_ada", [D, 6 * D])
    w_in = din("w_in", [D, 5120])
    w_out = din("w_out", [D, D])
    w_ff1 = din("w_ff1", [D, 4 * D])
    w_ff2 = din("w_ff2", [4 * D, D])
    b_adaT = din("b_adaT", [128, 96])
    b_ada_g = din("b_ada_g", [2, D])
    b_gluT = din("b_gluT", [128, 16])
    lam4 = din("lam4", [4, 64])
    subg = din("subg", [1, 128])
    w_dwT = din("w_dwT", [128, 8 * 31])
    cvec = din("cvec", [128, 24])
    b_out = din("b_out", [1, D])
    ln1 = din("ln1", [2, D])
    b1T = din("b1T", [128, 64])
    b_ff2 = din("b_ff2", [1, D])
    ln2 = din("ln2", [2, D])
    rope = din("rope", [4, 128, T])
    hmask = din("hmask", [128, 2])
    consts = din("consts", [128, 384])
    out = nc.dram_tensor("out", [T, D], F32, kind="ExternalOutput").ap()
    skind = "ExternalOutput" if debug else "Internal"
    Qs = nc.dram_tensor("Qs", [8, 128, T], BF16, kind=skind).ap()
    Ks = nc.dram_tensor("Ks", [8, 128, NKEY], BF16, kind=skind).ap()
    Vs = nc.dram_tensor("Vs", [8, 128, 34, 128], BF16, kind=skind).ap()
    ACs = nc.dram_tensor("ACs", [16, 128, T], BF16, kind=skind).ap()

    A = Arena(nc, 229344)
    psm = nc.alloc_psum_tensor("psm", [128, 6, 512], F32)
    pst = nc.alloc_psum_tensor("pst", [128, 2, 1024], BF16)
    r_bank = [Res("bank%d" % i) for i in range(8)]

    def bank32(i):
        return psm[:, i, :] if i < 6 else pst[:, i - 6, :].bitcast(F32)

    def bank16(i):
        return psm[:, i, :].bitcast(BF16) if i < 6 else pst[:, i - 6, :]

    ident = A.alloc("ident", [128, 128], BF16)
    permm = A.alloc("permm", [128, 128], BF16)
    ones_bf = A.alloc("ones_bf", [128, 128], BF16)
    ones32 = A.alloc("ones32", [128, 128], F32)
    modv = A.alloc("modv", [128, 96, 2], F32)
    g_bc = [A.alloc("g1bc", [128, D], BF16), A.alloc("g2bc", [128, D], BF16)]
    bglu = A.alloc("bglu", [128, 16], F32)
    cv = A.alloc("cv", [128, 24], F32)
    b1s = A.alloc("b1s", [128, 64], F32)
    hm = A.alloc("hm", [128, 2], F32)
    wdw = A.alloc("wdw", [128, 8 * 31], F32)
    lamv = A.alloc("lamv", [128, 8], F32)
    gsub = A.alloc("gsub", [128, 128], F32)
    wbufs = [A.alloc("wblk", [128, 16, 256], BF16) for _ in range(3)]
    wpool = Pool([(wbufs[i], [Res("w%da" % i), Res("w%db" % i)]) for i in range(3)])
    c2 = A.alloc("c2", [128, 32], F32)
    cs = A.alloc("cs", [128, 16, 2], BF16)
    csb = A.alloc("csb", [128, 16, 128], BF16)
    badaT = A.alloc("badaT", [128, 96], F32)
    acc_bank = Pool([0, 1, 2, 3])
    r_cst = Res("cst")
    r_mod = Res("mod")
    r_g = [Res("g1"), Res("g2")]
    r_lam = Res("lam")
    persist_end = A.off

    def wview(W):
        return W.rearrange("(kc p) n -> p kc n", p=128)

    def load_w(Wv, kc0, n0, halves=None):
        buf, rr = wpool.next()
        if halves is None:
            P.add("gpsimd", lambda e: e.dma_start(out=buf[:], in_=Wv[:, kc0:kc0 + 16, n0:n0 + 256]),
                  writes=rr, dma=True)
        else:
            for hh, nn in enumerate(halves):
                P.add("gpsimd", lambda e, hh=hh, nn=nn: e.dma_start(
                    out=buf[:, :, hh * 128:(hh + 1) * 128], in_=Wv[:, kc0:kc0 + 16, nn:nn + 128]),
                    writes=[rr[hh]], dma=True)
        return buf, rr

    def stream(specs, fn, depth=2):
        loaded = []
        n = len(specs)
        for i in range(min(depth, n)):
            loaded.append(load_w(*specs[i][1]))
        for i in range(n):
            buf, rr = loaded[i]
            fn(specs[i][0], buf, rr)
            if i + depth < n:
                loaded.append(load_w(*specs[i + depth][1]))

    ph_mark = A.off
    cst32 = A.alloc("cst32", [128, 384], F32)
    r_c32 = Res("c32")
    P.add("sync", lambda e: e.dma_start(out=cst32[:], in_=consts), writes=[r_c32], dma=True)
    P.add("vector", lambda e: e.tensor_copy(out=ident[:], in_=cst32[:, 0:128]), reads=[r_c32], writes=[r_cst])
    P.add("vector", lambda e: e.tensor_copy(out=permm[:], in_=cst32[:, 128:256]), reads=[r_c32], writes=[r_cst])
    P.add("vector", lambda e: e.tensor_copy(out=ones_bf[:], in_=cst32[:, 256:384]), reads=[r_c32], writes=[r_cst])
    P.add("vector", lambda e: e.tensor_copy(out=ones32[:], in_=cst32[:, 256:384]), reads=[r_c32], writes=[r_cst])
    for dst, src in ((bglu, b_gluT), (cv, cvec), (b1s, b1T), (hm, hmask), (wdw, w_dwT)):
        P.add("sync", lambda e, dst=dst, src=src: e.dma_start(out=dst[:], in_=src), writes=[r_cst], dma=True)
    P.add("sync", lambda e: e.dma_start(out=gsub[:], in_=subg.partition_broadcast(128).rearrange("p a b -> p (a b)")),
          writes=[r_cst], dma=True)
    lam_in = A.alloc("lam_in", [128, 4, 64], F32)
    lam_t = A.alloc("lam_t", [128, 2, 64], F32)
    P.add("sync", lambda e: e.dma_start(out=lam_in[:].rearrange("p a b -> p (a b)"),
                                         in_=lam4.rearrange("a b -> (a b)").partition_broadcast(128)),
          writes=[r_lam], dma=True)
    P.add("vector", lambda e: e.tensor_tensor(out=lam_t[:, 0, :], in0=lam_in[:, 0, :], in1=lam_in[:, 1, :], op=ALU.mult),
          writes=[r_lam])
    P.add("vector", lambda e: e.tensor_tensor(out=lam_t[:, 1, :], in0=lam_in[:, 2, :], in1=lam_in[:, 3, :], op=ALU.mult),
          writes=[r_lam])
    P.add("vector", lambda e: e.reduce_sum(out=lamv[:, 0:2], in_=lam_t[:], axis=mybir.AxisListType.X), writes=[r_lam])
    P.add("scalar", lambda e: e.activation(out=lamv[:, 3:5], in_=lamv[:, 0:2], func=AF.Exp), writes=[r_lam])
    P.add("vector", lambda e: e.scalar_tensor_tensor(out=lamv[:, 2:3], in0=lamv[:, 4:5], scalar=-LAM_INIT,
                                                     in1=lamv[:, 3:4], op0=ALU.add, op1=ALU.subtract), writes=[r_lam])
    P.add("vector", lambda e: e.tensor_scalar(out=gsub[:], in0=gsub[:], scalar1=1.0 - LAM_INIT, scalar2=None,
                                              op0=ALU.mult), writes=[r_cst])

    r_c2 = Res("c2")
    P.add("sync", lambda e: e.dma_start(out=c2[:], in_=cT2), writes=[r_c2], dma=True)
    P.add("sync", lambda e: e.dma_start(out=badaT[:], in_=b_adaT), writes=[r_c2], dma=True)
    P.add("scalar", lambda e: e.activation(out=cs[:].rearrange("p a b -> p (a b)"), in_=c2[:], func=AF.Silu),
          writes=[r_c2])
    P.add("vector", lambda e: e.tensor_copy(out=csb[:], in_=cs[:, :, 0:1].to_broadcast([128, 16, 128])),
          writes=[r_c2])
    wada_v = wview(w_ada)
    ada_bank = acc_bank

    def ada_fn(tag, buf, rr):
        kind, j0 = tag
        bk = ada_bank.next()
        if kind == "p":
            for cc in range(2):
                j = j0 + cc
                if cc == 1:
                    bk = ada_bank.next()
                for k in range(16):
                    P.add("tensor", lambda e, k=k, cc=cc, bk=bk: e.matmul(
                        bank32(bk)[:, 0:2], buf[:, k, cc * 128:(cc + 1) * 128], cs[:, k, :],
                        start=(k == 0), stop=(k == 15)), reads=rr + [r_c2], writes=[r_bank[bk]])
                addc = 1.0 if (16 <= j < 32 or 64 <= j < 80) else 0.0
                P.add("vector", lambda e, j=j, cc=cc, addc=addc, bk=bk: e.tensor_scalar(
                    out=modv[:, j, :], in0=bank32(bk)[:, 0:2], scalar1=badaT[:, j:j + 1],
                    scalar2=addc, op0=ALU.add, op1=ALU.add), reads=[r_bank[bk], r_c2], writes=[r_mod])
        else:
            gi, n0 = j0
            for k in range(16):
                P.add("tensor", lambda e, k=k: e.matmul(
                    bank32(bk)[:, 0:256], csb[:, k, :], buf[:, k, :], start=(k == 0), stop=(k == 15)),
                    reads=rr + [r_c2], writes=[r_bank[bk]])
            P.add("vector", lambda e: e.tensor_tensor(out=g_bc[gi][:, n0:n0 + 256], in0=bank32(bk)[:, 0:256],
                                                      in1=g_bc[gi][:, n0:n0 + 256], op=ALU.add),
                  reads=[r_bank[bk]], writes=[r_g[gi]])

    def ada_specs(chunks):
        return [(("p", j), (wada_v, 0, j * 128)) for j in chunks]

    stream(ada_specs(list(range(0, 32, 2))), ada_fn)
    ada_late = ada_specs(list(range(48, 80, 2)))
    for gi, base in ((0, 2 * D), (1, 5 * D)):
        for hh in range(2):
            P.add("gpsimd", lambda e, gi=gi, hh=hh: e.dma_start(
                out=g_bc[gi][:, hh * 1024:(hh + 1) * 1024],
                in_=b_ada_g[gi, hh * 1024:(hh + 1) * 1024].partition_broadcast(128)), writes=[r_g[gi]], dma=True)
        ada_late += [(("g", (gi, n0)), (wada_v, 0, base + n0)) for n0 in range(0, D, 256)]
    A.off = ph_mark

    hT = A.alloc("hT", [128, 16, 2304], BF16)
    hT_end = A.off
    r_hT = [Res("hT%d" % i) for i in range(5)]
    xin = [A.alloc("xin", [128, D], F32) for _ in range(3)]
    nrm = [A.alloc("nrm", [128, D], BF16) for _ in range(3)]
    r_xin = [Res("xin%d" % i) for i in range(3)]
    r_nrm = [Res("nrm%d" % i) for i in range(3)]
    stt = [A.alloc("stt", [128, 4, 6], F32) for _ in range(3)]
    mv = [A.alloc("mv", [128, 8], F32) for _ in range(3)]
    r_stat = [Res("stat%d" % i) for i in range(3)]
    gT_start = A.off
    gT = A.alloc("gT", [128, 8, 2080], BF16)
    r_gT = Res("gT")
    ropeb = A.alloc("ropeb", [128, 2, T], F32)
    r_rope = Res("rope")
    qb_p = Pool([(A.alloc("qb", [128, 512], BF16), Res("qb%d" % i)) for i in range(1)])
    t1_p = Pool([(A.alloc("t1", [128, 512], F32), Res("t1%d" % i)) for i in range(1)])
    t2_p = Pool([(A.alloc("t2", [128, 512], F32), Res("t2%d" % i)) for i in range(1)])
    qr_p = Pool([(A.alloc("qr", [128, 512], BF16), Res("qr%d" % i)) for i in range(2)])
    sg_p = Pool([(A.alloc("sg", [128, 512], BF16), Res("sg%d" % i)) for i in range(2)])
    vst_p = Pool([(A.alloc("vst", [128, 256], BF16), Res("vst%d" % i)) for i in range(2)])

    ln_i = [0]

    def ln_stats(src_ap_fn, stat, rsrc):
        stt_, mv_, rst_ = stat
        for c4 in range(4):
            P.add("vector", lambda e, c4=c4: e.bn_stats(out=stt_[:, c4, :], in_=src_ap_fn(c4)),
                  reads=[rsrc], writes=[rst_], loose=(c4 > 0))
        P.add("vector", lambda e: e.bn_aggr(out=mv_[:, 0:2], in_=stt_[:].rearrange("p a b -> p (a b)")),
              writes=[rst_])
        P.add("scalar", lambda e: e.activation(out=mv_[:, 2:3], in_=mv_[:, 1:2], func=AF.Sqrt, bias=EPS, scale=1.0),
              writes=[rst_])
        P.add("vector", lambda e: e.reciprocal(out=mv_[:, 3:4], in_=mv_[:, 2:3]), writes=[rst_])
        P.add("vector", lambda e: e.tensor_scalar(out=mv_[:, 4:5], in0=mv_[:, 0:1], scalar1=mv_[:, 3:4],
                                                  scalar2=-1.0, op0=ALU.mult, op1=ALU.mult), writes=[rst_])

    def transpose_mod(nrm_, rnrm_, tb, jsh, jsc, m, dst_fn, rdst):
        for k in range(16):
            bk = tb + k // 8
            P.add("tensor", lambda e, k=k, bk=bk: e.transpose(
                bank16(bk)[:, (k % 8) * 128:(k % 8 + 1) * 128], nrm_[:, k * 128:(k + 1) * 128], ident[:]),
                reads=[rnrm_, r_cst], writes=[r_bank[bk]])
        for k in range(16):
            bk = tb + k // 8
            src = bank16(bk)[:, (k % 8) * 128:(k % 8 + 1) * 128]
            if k < 8:
                P.add("scalar", lambda e, k=k, src=src: e.activation(
                    out=dst_fn(k), in_=src, func=AF.Identity, scale=modv[:, jsc + k, m:m + 1],
                    bias=modv[:, jsh + k, m:m + 1]), reads=[r_bank[bk], r_mod], writes=[rdst], loose=True)
            else:
                P.add("vector", lambda e, k=k, src=src: e.tensor_scalar(
                    out=dst_fn(k), in0=src, scalar1=modv[:, jsc + k, m:m + 1], scalar2=modv[:, jsh + k, m:m + 1],
                    op0=ALU.mult, op1=ALU.add), reads=[r_bank[bk], r_mod], writes=[rdst], loose=True)

    def ln_tiles(tiles):
        def stats(src, b):
            xb = xin[b]
            P.add("sync", lambda e: e.dma_start(out=xb[:], in_=src), writes=[r_xin[b]], dma=True)
            stt_, mv_, rst_ = stt[b], mv[b], r_stat[b]
            for c4 in range(4):
                P.add("vector", lambda e, c4=c4: e.bn_stats(out=stt_[:, c4, :], in_=xb[:, c4 * 512:(c4 + 1) * 512]),
                      reads=[r_xin[b]], writes=[rst_], loose=(c4 > 0))
            P.add("vector", lambda e: e.bn_aggr(out=mv_[:, 0:2], in_=stt_[:].rearrange("p a b -> p (a b)")),
                  writes=[rst_])

        def sqrt_(b):
            mv_, rst_ = mv[b], r_stat[b]
            P.add("scalar", lambda e: e.activation(out=mv_[:, 2:3], in_=mv_[:, 1:2], func=AF.Sqrt, bias=EPS, scale=1.0),
                  writes=[rst_])

        def rstd_norm(b):
            xb, nb, mv_, rst_ = xin[b], nrm[b], mv[b], r_stat[b]
            P.add("vector", lambda e: e.reciprocal(out=mv_[:, 3:4], in_=mv_[:, 2:3]), writes=[rst_])
            P.add("vector", lambda e: e.tensor_scalar(out=mv_[:, 4:5], in0=mv_[:, 0:1], scalar1=mv_[:, 3:4],
                                                      scalar2=-1.0, op0=ALU.mult, op1=ALU.mult), writes=[rst_])
            P.add("scalar", lambda e: e.activation(out=nb[:], in_=xb[:], func=AF.Identity,
                                                   scale=mv_[:, 3:4], bias=mv_[:, 4:5]),
                  reads=[r_xin[b], rst_], writes=[r_nrm[b]])

        def stage_b(m, col, b):
            transpose_mod(nrm[b], r_nrm[b], 2 * b, 0, 16, m, lambda k: hT[:, k, col:col + 128], r_hT[col // 512])
        n = len(tiles)
        base = ln_i[0]
        ln_i[0] += n
        stats(tiles[0][0], base % 3)
        sqrt_(base % 3)
        for step in range(n + 1):
            i1 = step
            i2 = step + 1
            i0 = step - 1
            if i1 < n:
                rstd_norm((base + i1) % 3)
            if i2 < n:
                stats(tiles[i2][0], (base + i2) % 3)
            if i0 >= 0:
                _, m, col = tiles[i0]
                stage_b(m, col, (base + i0) % 3)
            if i2 < n:
                sqrt_((base + i2) % 3)

    win_v = wview(w_in)

    def rope_store(ps_bank, rbank, tt, dst_ap):
        qb, rqb = qb_p.next()
        t1, rt1 = t1_p.next()
        t2, rt2 = t2_p.next()
        qr, rqr = qr_p.next()
        pqb = 4 + (tt % 2)
        P.add("scalar", lambda e: e.activation(out=qb[:], in_=bank32(ps_bank), func=AF.Identity), reads=[rbank], writes=[rqb])
        P.add("tensor", lambda e: e.matmul(bank32(pqb), permm[:], qb[:], start=True, stop=True),
              reads=[rqb, r_cst], writes=[r_bank[pqb]])
        P.add("vector", lambda e: e.tensor_tensor(out=t1[:], in0=bank32(ps_bank), in1=ropeb[:, 0, tt * 512:(tt + 1) * 512],
                                                  op=ALU.mult), reads=[rbank, r_rope, rqb], writes=[rt1])
        P.add("vector", lambda e: e.tensor_tensor(out=t2[:], in0=bank32(pqb), in1=ropeb[:, 1, tt * 512:(tt + 1) * 512],
                                                  op=ALU.mult), reads=[r_bank[pqb], r_rope], writes=[rt2])
        P.add("vector", lambda e: e.tensor_tensor(out=qr[:], in0=t1[:], in1=t2[:], op=ALU.add),
              reads=[rt1, rt2], writes=[rqr])
        P.add("sync", lambda e: e.dma_start(out=dst_ap, in_=qr[:]), reads=[rqr], dma=True)

    pend_rope = []

    def rope_defer(*args):
        pend_rope.append(args)
        if len(pend_rope) > 1:
            rope_store(*pend_rope.pop(0))

    def rope_flush():
        while pend_rope:
            rope_store(*pend_rope.pop(0))

    def proj_fm(buf, rr, cc, tok0, ntok, rh):
        bk = acc_bank.next()
        for k in range(16):
            P.add("tensor", lambda e, k=k: e.matmul(bank32(bk)[:, 0:ntok], buf[:, k, cc * 128:(cc + 1) * 128],
                                                    hT[:, k, tok0:tok0 + ntok], start=(k == 0), stop=(k == 15)),
                  reads=rr + rh, writes=[r_bank[bk]])
        return bk

    def v_tm(buf, rr, tile, chunk, h0, rh):
        bk = acc_bank.next()
        for k in range(16):
            P.add("tensor", lambda e, k=k: e.matmul(bank32(bk)[:, 0:256], hT[:, k, tile * 128:(tile + 1) * 128],
                                                    buf[:, k, :], start=(k == 0), stop=(k == 15)),
                  reads=rr + rh, writes=[r_bank[bk]])
        vst, rv = vst_p.next()
        P.add("scalar", lambda e: e.activation(out=vst[:], in_=bank32(bk)[:, 0:256], func=AF.Identity),
              reads=[r_bank[bk]], writes=[rv])
        P.add("sync", lambda e: e.dma_start(out=Vs[h0:h0 + 2, :, chunk, :].rearrange("h p e -> p h e"),
                                             in_=vst[:].rearrange("p (h e) -> p h e", h=2)), reads=[rv], dma=True)

    def glu_store(bk_a, bk_g, c, ncol, dst_ap, mask_ap=None):
        sg, rsg = sg_p.next()
        P.add("scalar", lambda e: e.activation(out=sg[:, 0:ncol], in_=bank32(bk_g)[:, 0:ncol], func=AF.Sigmoid,
                                               bias=bglu[:, 8 + c:9 + c], scale=1.0),
              reads=[r_bank[bk_g], r_cst], writes=[rsg])
        if mask_ap is None:
            P.add("vector", lambda e: e.scalar_tensor_tensor(out=dst_ap, in0=bank32(bk_a)[:, 0:ncol],
                                                             scalar=bglu[:, c:c + 1], in1=sg[:, 0:ncol],
                                                             op0=ALU.add, op1=ALU.mult),
                  reads=[r_bank[bk_a], rsg, r_cst], writes=[r_gT])
        else:
            P.add("vector", lambda e: e.scalar_tensor_tensor(out=sg[:, 0:ncol], in0=bank32(bk_a)[:, 0:ncol],
                                                             scalar=bglu[:, c:c + 1], in1=sg[:, 0:ncol],
                                                             op0=ALU.add, op1=ALU.mult),
                  reads=[r_bank[bk_a], r_cst], writes=[rsg])
            for hh in range(2):
                P.add("vector", lambda e, hh=hh: e.tensor_scalar(
                    out=dst_ap[hh], in0=sg[:, hh * 16:(hh + 1) * 16], scalar1=mask_ap[hh], scalar2=None, op0=ALU.mult),
                    reads=[rsg, r_cst], writes=[r_gT])

    xo_v = x_own.rearrange("(t p) d -> t p d", p=128)
    ln_tiles([(xo_v[t], 0, t * 128) for t in range(16)])
    P.add("sync", lambda e: e.dma_start(out=ropeb[:], in_=rope[0:2].rearrange("a p t -> p a t")), writes=[r_rope], dma=True)
    P.add("vector", lambda e: e.memset(gT[:].rearrange("p a b -> p (a b)"), 0.0), writes=[r_gT])

    rh_own = r_hT[0:4]

    def a2_fn(tag, buf, rr):
        kind, j = tag
        if kind in ("q", "k"):
            for cc in range(2):
                h = 2 * j + cc
                for tt in range(4):
                    bk = proj_fm(buf, rr, cc, tt * 512, 512, [r_hT[tt]])
                    dst = (Qs if kind == "q" else Ks)[h, :, tt * 512:(tt + 1) * 512]
                    rope_defer(bk, r_bank[bk], tt, dst)
            rope_flush()
        elif kind == "v":
            for tile in range(16):
                v_tm(buf, rr, tile, tile, 2 * j, [r_hT[tile // 4]])
        else:
            for tt in range(4):
                bk_a = proj_fm(buf, rr, 0, tt * 512, 512, [r_hT[tt]])
                bk_g = proj_fm(buf, rr, 1, tt * 512, 512, [r_hT[tt]])
                glu_store(bk_a, bk_g, j, 512, gT[:, j, 16 + tt * 512:16 + (tt + 1) * 512])

    specs = [(("q", j), (win_v, 0, j * 256)) for j in range(4)]
    specs += [(("k", j), (win_v, 0, 1024 + j * 256)) for j in range(4)]
    specs += [(("v", j), (win_v, 0, 2048 + j * 256)) for j in range(4)]
    specs += [(("u", c), (win_v, 0, 0, (3072 + c * 128, 4096 + c * 128))) for c in range(8)]
    merged = []
    for sp in specs:
        merged.append(sp)
        for _ in range(2):
            if ada_late:
                merged.append(ada_late.pop(0))
    merged += ada_late

    def a2_dispatch(tag, buf, rr):
        if tag[0] in ("p", "g"):
            ada_fn(tag, buf, rr)
        else:
            a2_fn(tag, buf, rr)
    stream(merged, a2_dispatch)

    xt_v = x_oth.rearrange("(t p) d -> t p d", p=128)
    cx_v = ctxb.rearrange("(t p) d -> t p d", p=128)
    ln_tiles([(xt_v[t], 0, t * 128) for t in range(16)] + [(cx_v[t], 1, 2048 + t * 128) for t in range(2)])
    P.add("sync", lambda e: e.dma_start(out=ropeb[:], in_=rope[2:4].rearrange("a p t -> p a t")), writes=[r_rope], dma=True)

    def a4_fn(tag, buf, rr):
        kind, j = tag
        if kind == "k":
            for cc in range(2):
                h = 2 * j + cc
                for tt in range(4):
                    bk = proj_fm(buf, rr, cc, tt * 512, 512, [r_hT[tt]])
                    rope_defer(bk, r_bank[bk], tt, Ks[h, :, 2048 + tt * 512:2048 + (tt + 1) * 512])
                bk = proj_fm(buf, rr, cc, 2048, 256, [r_hT[4]])
                rope_flush()
                qr, rqr = qr_p.next()
                P.add("scalar", lambda e, bk=bk, qr=qr: e.activation(out=qr[:, 0:256], in_=bank32(bk)[:, 0:256], func=AF.Identity),
                      reads=[r_bank[bk]], writes=[rqr])
                P.add("sync", lambda e, h=h, qr=qr: e.dma_start(out=Ks[h, :, 4096:4352], in_=qr[:, 0:256]), reads=[rqr], dma=True)
        elif kind == "v":
            for tile in range(18):
                v_tm(buf, rr, tile, 16 + tile, 2 * j, [r_hT[tile // 4]])
        else:
            bk_a = acc_bank.next()
            bk_g = acc_bank.next()
            for (bk, cc) in ((bk_a, 0), (bk_g, 1)):
                for hh, t0 in enumerate((0, 2032)):
                    for k in range(16):
                        P.add("tensor", lambda e, k=k, bk=bk, cc=cc, hh=hh, t0=t0: e.matmul(
                            bank32(bk)[:, hh * 16:(hh + 1) * 16], buf[:, k, cc * 128:(cc + 1) * 128],
                            hT[:, k, t0:t0 + 16], start=(k == 0), stop=(k == 15)),
                            reads=rr + [r_hT[0], r_hT[3]], writes=[r_bank[bk]])
            glu_store(bk_a, bk_g, j, 32, [gT[:, j, 2064:2080], gT[:, j, 0:16]], mask_ap=[hm[:, 1:2], hm[:, 0:1]])

    specs = [(("k", j), (win_v, 0, 1024 + j * 256)) for j in range(4)]
    specs += [(("v", j), (win_v, 0, 2048 + j * 256)) for j in range(4)]
    specs += [(("u", c), (win_v, 0, 0, (3072 + c * 128, 4096 + c * 128))) for c in range(8)]
    stream(specs, a4_fn)
    P.barrier()

    a5_mark = A.off
    A.off = ph_mark
    diag_all = A.alloc("diag", [128, 8, 31, 128], BF16)
    r_diag = [Res("diag%d" % i) for i in range(8)]
    c32 = A.alloc("c32", [128, 8, 512], F32)
    r_c32b = Res("c32b")
    csq_p = Pool([(A.alloc("csq", [128, 512], F32), Res("csq%d" % i)) for i in range(2)])
    meanb = A.alloc("meanb", [128, 512], F32)
    msq = A.alloc("msq", [128, 512], F32)
    rstdb = A.alloc("rstdb", [128, 512], F32)
    r_cstat = Res("cstat")
    ct_p = Pool([(A.alloc("ct", [128, 512], F32), Res("ct%d" % i)) for i in range(2)])
    co_p = Pool([(A.alloc("co", [128, 512], BF16), Res("co%d" % i)) for i in range(2)])
    assert A.off <= gT_start
    for c in range(8):
        for j in range(31):
            P.add("vector", lambda e, c=c, j=j: e.tensor_scalar(
                out=diag_all[:, c, j, :], in0=ident[:], scalar1=wdw[:, c * 31 + j:c * 31 + j + 1], scalar2=None,
                op0=ALU.mult), reads=[r_cst], writes=[r_diag[c]], loose=True)
    di = [0]
    pend_stats = []
    for tt in range(4):
        for c in range(8):
            bk = c % 2
            for j in range(31):
                P.add("tensor", lambda e, c=c, j=j, bk=bk, tt=tt: e.matmul(
                    bank32(bk), diag_all[:, c, j, :], gT[:, c, tt * 512 + j + 1:tt * 512 + j + 513],
                    start=(j == 0), stop=(j == 30)), reads=[r_diag[c], r_gT], writes=[r_bank[bk]])
            csq, rcsq = csq_p.next()
            P.add("scalar", lambda e, c=c, bk=bk: e.activation(out=c32[:, c, :], in_=bank32(bk), func=AF.Identity,
                                                               bias=cv[:, c:c + 1], scale=1.0),
                  reads=[r_bank[bk], r_cst], writes=[r_c32b], loose=True)
            P.add("scalar", lambda e, c=c, bk=bk, csq=csq: e.activation(out=csq[:], in_=bank32(bk), func=AF.Square,
                                                                       bias=cv[:, c:c + 1], scale=1.0),
                  reads=[r_bank[bk], r_cst], writes=[rcsq])
            def stats_mm(c=c, csq=csq, rcsq=rcsq):
                P.add("tensor", lambda e: e.matmul(bank32(4), ones32[:], c32[:, c, :], start=(c == 0), stop=(c == 7)),
                      reads=[r_c32b, r_cst], writes=[r_bank[4]])
                P.add("tensor", lambda e: e.matmul(bank32(5), ones32[:], csq[:], start=(c == 0), stop=(c == 7)),
                      reads=[rcsq, r_cst], writes=[r_bank[5]])
            pend_stats.append(stats_mm)
            if len(pend_stats) > 1:
                pend_stats.pop(0)()
        while pend_stats:
            pend_stats.pop(0)()
        P.add("scalar", lambda e: e.activation(out=meanb[:], in_=bank32(4), func=AF.Identity, scale=1.0 / 1024),
              reads=[r_bank[4]], writes=[r_cstat])
        P.add("vector", lambda e: e.tensor_tensor(out=msq[:], in0=meanb[:], in1=meanb[:], op=ALU.mult), writes=[r_cstat])
        P.add("vector", lambda e: e.scalar_tensor_tensor(out=msq[:], in0=bank32(5), scalar=1.0 / 1024, in1=msq[:],
                                                         op0=ALU.mult, op1=ALU.subtract), reads=[r_bank[5]], writes=[r_cstat])
        P.add("scalar", lambda e: e.activation(out=msq[:], in_=msq[:], func=AF.Sqrt, bias=EPS, scale=1.0), writes=[r_cstat])
        P.add("vector", lambda e: e.reciprocal(out=rstdb[:], in_=msq[:]), writes=[r_cstat])
        for c in range(8):
            ct, rct = ct_p.next()
            co, rco = co_p.next()
            P.add("vector", lambda e, c=c, ct=ct: e.tensor_tensor(out=ct[:], in0=c32[:, c, :], in1=meanb[:], op=ALU.subtract),
                  reads=[r_c32b, r_cstat], writes=[rct])
            P.add("vector", lambda e, ct=ct: e.tensor_tensor(out=ct[:], in0=ct[:], in1=rstdb[:], op=ALU.mult),
                  reads=[r_cstat], writes=[rct])
            P.add("scalar", lambda e, c=c, ct=ct, co=co: e.activation(out=co[:], in_=ct[:], func=AF.Silu,
                                                                      scale=cv[:, 8 + c:9 + c], bias=cv[:, 16 + c:17 + c]),
                  reads=[rct, r_cst], writes=[rco])
            P.add("sync", lambda e, c=c, co=co, tt=tt: e.dma_start(out=ACs[8 + c, :, tt * 512:(tt + 1) * 512], in_=co[:]),
                  reads=[rco], dma=True)
    P.barrier()
    A.off = ph_mark
    if stop_after == "A":
        return _finish(nc, P, out, A)

    Kh = [A.alloc("Kh", [128, NKEY], BF16) for _ in range(2)]
    Vh = [A.alloc("Vh", [128, 34, 130], BF16) for _ in range(2)]
    Qh = [A.alloc("Qh", [128, T], BF16) for _ in range(2)]
    r_k = [Res("k0"), Res("k1")]
    r_q = [Res("q0"), Res("q1")]
    r_v = [Res("v0"), Res("v1")]
    Pt_p = Pool([(A.alloc("Pt", [128, 1024], BF16), Res("Pt%d" % i)) for i in range(3)])
    attn = A.alloc("attn", [128, 16, 1024], BF16)
    r_attn = Res("attn")
    ob_p = Pool([(A.alloc("ob", [128, 128], F32), Res("ob%d" % i)) for i in range(2)])
    junk = A.alloc("junk", [128, 128], F32)
    rsb = A.alloc("rsb", [128, 16], F32)
    r_rs = Res("rs")
    aT_p = Pool([(A.alloc("aT", [128, 8, 128], BF16), Res("aT%d" % i)) for i in range(2)])
    for b in range(2):
        P.add("vector", lambda e, b=b: e.memset(Vh[b][:].rearrange("p a b -> p (a b)"), 1.0), writes=[r_v[b]])

    def load_head(h):
        b = h % 2
        P.add("sync", lambda e: e.dma_start(out=Kh[b][:], in_=Ks[h]), writes=[r_k[b]], dma=True)
        P.add("sync", lambda e: e.dma_start(out=Qh[b][:], in_=Qs[h]), writes=[r_q[b]], dma=True)
        P.add("sync", lambda e: e.dma_start(out=Vh[b][:, :, 0:128], in_=Vs[h]), writes=[r_v[b]], dma=True)

    SC = 0.125
    def ogrp(g):
        bk = 4 + g // 3
        off = (g % 3) * 160
        return bk, bank32(bk)[:, off:off + 129]

    oc_p = Pool([(A.alloc("oc", [128, 3, 512], F32), Res("oc%d" % i)) for i in range(2)])
    ob4 = [A.alloc("ob4", [128, 128], F32) for _ in range(4)]
    r_ob4 = [Res("ob4_%d" % i) for i in range(4)]
    iters = [(h, qb, kc) for h in range(8) for qb in range(4) for kc in range(34)]

    def emit_qk(i):
        h, qb, kc = iters[i]
        b = h % 2
        sb = 2 * (i % 2)
        K_, Q_ = Kh[b], Qh[b]
        for c in range(2):
            P.add("tensor", lambda e, c=c: e.matmul(
                bank32(sb + c), K_[c * 64:(c + 1) * 64, kc * 128:(kc + 1) * 128],
                Q_[c * 64:(c + 1) * 64, qb * 512:(qb + 1) * 512], start=True, stop=True),
                reads=[r_k[b], r_q[b]], writes=[r_bank[sb + c]])

    def emit_exp_pv(i):
        h, qb, kc = iters[i]
        b = h % 2
        sb = 2 * (i % 2)
        V_ = Vh[b]
        Pt, rPt = Pt_p.next()
        P.add("scalar", lambda e: e.activation(
            out=Pt[:], in_=psm[:, sb:sb + 2, :].rearrange("p a b -> p (a b)"), func=AF.Exp, scale=SC),
            reads=[r_bank[sb], r_bank[sb + 1]], writes=[rPt])
        for g in range(8):
            c, qs = g // 4, g % 4
            bk, oap = ogrp(g)
            P.add("tensor", lambda e, g=g, c=c, qs=qs, oap=oap: e.matmul(
                oap, Pt[:, c * 512 + qs * 128:c * 512 + (qs + 1) * 128], V_[:, kc, 0:129],
                start=(kc == 0 and g % 3 == 0), stop=(kc == 33), skip_group_check=True),
                reads=[rPt, r_v[b]], writes=[r_bank[bk]])

    def emit_evac(h, qb):
        oc, roc = oc_p.next()
        for g in range(8):
            j, o_ = g // 3, (g % 3) * 160
            P.add("vector", lambda e, j=j, o_=o_: e.tensor_copy(out=oc[:, j, o_:o_ + 129], in_=bank32(4 + j)[:, o_:o_ + 129]),
                  reads=[r_bank[4 + j]], writes=[roc], loose=(g > 0))
        for j in range(3):
            ng = 3 if j < 2 else 2
            P.add("vector", lambda e, j=j, ng=ng: e.reciprocal(
                out=rsb[:, 8 + 3 * j:8 + 3 * j + ng],
                in_=oc[:, j, 0:480].rearrange("p (g c) -> p g c", c=160)[:, 0:ng, 128:129]),
                reads=[roc], writes=[r_rs])
        P.add("vector", lambda e: e.tensor_scalar(out=rsb[:, 12:16], in0=rsb[:, 12:16], scalar1=lamv[:, 2:3], scalar2=None,
                                                  op0=ALU.mult), reads=[r_lam], writes=[r_rs])

        def og(g):
            return oc[:, g // 3, (g % 3) * 160:(g % 3) * 160 + 128]
        for qs in range(4):
            ob = ob4[qs]
            P.add("vector", lambda e, ob=ob, qs=qs: e.tensor_scalar(
                out=ob[:], in0=og(qs), scalar1=rsb[:, 8 + qs:9 + qs], scalar2=None, op0=ALU.mult),
                reads=[roc, r_rs], writes=[r_ob4[qs]])
            P.add("vector", lambda e, ob=ob, qs=qs: e.scalar_tensor_tensor(
                out=ob[:], in0=og(4 + qs), scalar=rsb[:, 12 + qs:13 + qs], in1=ob[:], op0=ALU.mult, op1=ALU.add),
                reads=[roc, r_rs], writes=[r_ob4[qs]])
            P.add("vector", lambda e, ob=ob, qs=qs: e.scalar_tensor_tensor(
                out=junk[:], in0=ob[:], scalar=1.0, in1=ob[:], op0=ALU.mult, op1=ALU.mult,
                accum_out=rsb[:, qs:qs + 1]), reads=[r_ob4[qs]], writes=[r_rs])
        def tail():
            P.add("scalar", lambda e: e.activation(out=rsb[:, 0:4], in_=rsb[:, 0:4], func=AF.Sqrt, bias=EPS, scale=1.0 / 128),
                  writes=[r_rs])
            P.add("vector", lambda e: e.reciprocal(out=rsb[:, 4:8], in_=rsb[:, 0:4]), writes=[r_rs])
            for qs in range(4):
                ob = ob4[qs]
                qt = qb * 4 + qs
                P.add("vector", lambda e, ob=ob, qs=qs, qt=qt: e.scalar_tensor_tensor(
                    out=attn[:, qt, h * 128:(h + 1) * 128], in0=ob[:], scalar=rsb[:, 4 + qs:5 + qs], in1=gsub[:],
                    op0=ALU.mult, op1=ALU.mult), reads=[r_ob4[qs], r_rs, r_cst], writes=[r_attn])
        return tail

    _save_off = A.off
    A.off = A.limit - (2 * D * 4 + 4 * D * 2) - 64
    lng = [A.alloc("ln1g", [128, D], F32), A.alloc("ln2g", [128, D], F32)]
    lnb = [A.alloc("ln1b", [128, D], BF16), A.alloc("ln2b", [128, D], BF16)]
    brow = [A.alloc("bo_row", [128, D], BF16), A.alloc("b2_row", [128, D], BF16)]
    top_reserved = A.limit - (2 * D * 4 + 4 * D * 2) - 64
    A.off = _save_off
    assert A.off <= top_reserved
    r_ln = Res("ln")
    for i, src in enumerate((ln1, ln2)):
        P.add("sync", lambda e, i=i, src=src: e.dma_start(out=lng[i][:], in_=src[0].partition_broadcast(128)), writes=[r_ln], dma=True)
        P.add("gpsimd", lambda e, i=i, src=src: e.dma_start(out=lnb[i][:, 0:1024], in_=src[1, 0:1024].partition_broadcast(128)),
              writes=[r_ln], dma=True)
        P.add("gpsimd", lambda e, i=i, src=src: e.dma_start(out=lnb[i][:, 1024:2048], in_=src[1, 1024:2048].partition_broadcast(128)),
              writes=[r_ln], dma=True)
    for i, src in enumerate((b_out, b_ff2)):
        P.add("vector", lambda e, i=i: e.memset(brow[i][:], 0.0), writes=[r_ln])
        for hh in range(2):
            P.add("gpsimd", lambda e, i=i, src=src, hh=hh: e.dma_start(out=brow[i][0:1, hh * 1024:(hh + 1) * 1024],
                                                                      in_=src[0:1, hh * 1024:(hh + 1) * 1024]),
                  writes=[r_ln], dma=True)
    pend_tail = []
    load_head(0)
    load_head(1)
    emit_qk(0)
    for i, (h, qb, kc) in enumerate(iters):
        if i + 1 < len(iters):
            emit_qk(i + 1)
        emit_exp_pv(i)
        if kc == 3 and pend_tail:
            pend_tail.pop(0)()
        if kc == 33:
            pend_tail.append(emit_evac(h, qb))
            if qb == 3 and h + 2 < 8:
                load_head(h + 2)
    while pend_tail:
        pend_tail.pop(0)()
    for qt in range(16):
        bk = qt % 2
        aT, raT = aT_p.next()
        for k in range(8):
            P.add("tensor", lambda e, k=k, qt=qt, bk=bk: e.transpose(
                bank16(bk)[:, k * 128:(k + 1) * 128], attn[:, qt, k * 128:(k + 1) * 128], ident[:]),
                reads=[r_attn, r_cst], writes=[r_bank[bk]])
        P.add("vector", lambda e, aT=aT, bk=bk: e.tensor_copy(out=aT[:].rearrange("p a b -> p (a b)"), in_=bank16(bk)),
              reads=[r_bank[bk]], writes=[raT])
        P.add("sync", lambda e, aT=aT, qt=qt: e.dma_start(out=ACs[0:8, :, qt * 128:(qt + 1) * 128].rearrange("k p t -> p k t"),
                                                          in_=aT[:]), reads=[raT], dma=True)
    P.barrier()
    A.off = ph_mark
    if stop_after == "B":
        return _finish(nc, P, out, A)

    act = A.alloc("act", [128, 16, 512], BF16)
    r_act = Res("act")
    x1 = A.alloc("x1", [128, 4, D], F32)
    r_x1 = [Res("x1_%d" % i) for i in range(4)]
    hid_off = A.off
    hid = A.alloc("hid", [128, 64, 512], BF16)
    yb = nc.alloc_sbuf_tensor_at("yb_alias", [128, 4, D], F32, offset=hid_off)
    r_hid = Res("hid")
    nrmCs = [A.alloc("nrmc", [128, D], BF16) for _ in range(2)]
    r_nrmCs = [Res("nrmc0"), Res("nrmc1")]
    statCs = [(A.alloc("sttc", [128, 4, 6], F32), A.alloc("mvc", [128, 8], F32), Res("statc%d" % i)) for i in range(3)]
    statT = [[(A.alloc("sttA", [128, 4, 6], F32), A.alloc("mvA", [128, 8], F32), Res("stT%d_%d" % (t_, u_)))
              for u_ in range(2)] for t_ in range(4)]
    stat_i = [0]

    def next_stat():
        stat_i[0] += 1
        return statCs[stat_i[0] % 3]

    tmp_p = Pool([(A.alloc("tmpc", [128, 256], F32), Res("tmpc%d" % i)) for i in range(2)])
    rl_p = Pool([(A.alloc("rl", [128, 512], F32), Res("rl%d" % i)) for i in range(2)])
    assert A.off <= top_reserved, (A.off, top_reserved)
    wout_v = wview(w_out)
    w1_v = wview(w_ff1)
    w2_v = wview(w_ff2)
    out_v = out.rearrange("(t p) d -> t p d", p=128)
    acc6 = Pool([0, 1, 2, 3, 6, 7])

    def ln_apply(src_tile_fn, gi, dst_tile_fn, rsrc, rdst, rdst_halves=None):
        stC = next_stat()
        mvC = stC[1]
        ln_stats(lambda c4: src_tile_fn()[:, c4 * 512:(c4 + 1) * 512], stC, rsrc)
        P.add("scalar", lambda e: e.activation(out=dst_tile_fn(), in_=src_tile_fn(), func=AF.Identity,
                                               scale=mvC[:, 3:4], bias=mvC[:, 4:5]),
              reads=[rsrc, stC[2]], writes=[rdst])
        P.add("vector", lambda e: e.tensor_tensor(out=dst_tile_fn(), in0=dst_tile_fn(), in1=lng[gi][:], op=ALU.mult),
              reads=[r_ln], writes=[rdst])
        P.add("vector", lambda e: e.tensor_tensor(out=dst_tile_fn(), in0=dst_tile_fn(), in1=lnb[gi][:], op=ALU.add),
              reads=[r_ln], writes=[rdst])

    def load_act(tb):
        P.add("sync", lambda e: e.dma_start(out=act[:], in_=ACs[:, :, tb * 512:(tb + 1) * 512].rearrange("k p t -> p k t")),
              writes=[r_act], dma=True)

    def final_ln(tb, tt):
        ln_apply(lambda: x1[:, tt, :], 1, lambda: x1[:, tt, :], r_x1[tt], r_x1[tt])
        P.add("sync", lambda e: e.dma_start(out=out_v[tb * 4 + tt], in_=x1[:, tt, :]), reads=[r_x1[tt]], dma=True)

    def outproj(prev_tb):
        def op_fn(tag, buf, rr):
            cb = tag
            if prev_tb is not None and cb % 2 == 0:
                final_ln(prev_tb, cb // 2)
            for tt in range(4):
                bk = acc_bank.next()
                P.add("tensor", lambda e, bk=bk: e.matmul(bank32(bk)[:, 0:256], ones_bf[:], brow[0][:, cb * 256:(cb + 1) * 256],
                                                          start=True, stop=False), reads=[r_ln, r_cst], writes=[r_bank[bk]])
                for k in range(16):
                    P.add("tensor", lambda e, k=k, bk=bk, tt=tt: e.matmul(
                        bank32(bk)[:, 0:256], act[:, k, tt * 128:(tt + 1) * 128], buf[:, k, :], start=False, stop=(k == 15)),
                        reads=rr + [r_act], writes=[r_bank[bk]])
                P.add("vector", lambda e, bk=bk, tt=tt: e.tensor_tensor(
                    out=yb[:, tt, cb * 256:(cb + 1) * 256], in0=bank32(bk)[:, 0:256], in1=g_bc[0][:, cb * 256:(cb + 1) * 256],
                    op=ALU.mult), reads=[r_bank[bk], r_g[0]], writes=[r_hid], loose=True)
        stream([(cb, (wout_v, 0, cb * 256)) for cb in range(8)], op_fn)

    load_act(0)
    outproj(None)
    for tb in range(4):
        for tt in range(4):
            P.add("sync", lambda e, tb=tb, tt=tt: e.dma_start(out=x1[:, tt, :], in_=xo_v[tb * 4 + tt]), writes=[r_x1[tt]], dma=True)
            P.add("vector", lambda e, tt=tt: e.scalar_tensor_tensor(
                out=x1[:, tt, :], in0=x1[:, tt, :], scalar=ALPHA, in1=yb[:, tt, :], op0=ALU.mult, op1=ALU.add),
                reads=[r_hid], writes=[r_x1[tt]])

        def c_stage(sg, tt):
            xt = x1[:, tt, :]
            stA, stB = statT[tt]
            nrmC, r_nrmC = nrmCs[tt % 2], r_nrmCs[tt % 2]

            def stats_(st):
                stt_, mv_, rst_ = st
                for c4 in range(4):
                    P.add("vector", lambda e, c4=c4: e.bn_stats(out=stt_[:, c4, :], in_=x1[:, tt, c4 * 512:(c4 + 1) * 512]),
                          reads=[r_x1[tt]], writes=[rst_], loose=(c4 > 0))
                P.add("vector", lambda e: e.bn_aggr(out=mv_[:, 0:2], in_=stt_[:].rearrange("p a b -> p (a b)")), writes=[rst_])
                P.add("scalar", lambda e: e.activation(out=mv_[:, 2:3], in_=mv_[:, 1:2], func=AF.Sqrt, bias=EPS, scale=1.0),
                      writes=[rst_])

            def rstd_(st):
                stt_, mv_, rst_ = st
                P.add("vector", lambda e: e.reciprocal(out=mv_[:, 3:4], in_=mv_[:, 2:3]), writes=[rst_])
                P.add("vector", lambda e: e.tensor_scalar(out=mv_[:, 4:5], in0=mv_[:, 0:1], scalar1=mv_[:, 3:4],
                                                          scalar2=-1.0, op0=ALU.mult, op1=ALU.mult), writes=[rst_])
            if sg == 0:
                stats_(stA)
            elif sg == 1:
                rstd_(stA)
                P.add("scalar", lambda e: e.activation(out=xt, in_=xt, func=AF.Identity, scale=stA[1][:, 3:4], bias=stA[1][:, 4:5]),
                      reads=[stA[2]], writes=[r_x1[tt]])
            elif sg == 2:
                P.add("vector", lambda e: e.tensor_tensor(out=xt, in0=xt, in1=lng[0][:], op=ALU.mult), reads=[r_ln], writes=[r_x1[tt]])
                P.add("vector", lambda e: e.tensor_tensor(out=xt, in0=xt, in1=lnb[0][:], op=ALU.add), reads=[r_ln], writes=[r_x1[tt]])
            elif sg == 3:
                stats_(stB)
            elif sg == 4:
                rstd_(stB)
                P.add("scalar", lambda e: e.activation(out=nrmC[:], in_=xt, func=AF.Identity, scale=stB[1][:, 3:4], bias=stB[1][:, 4:5]),
                      reads=[r_x1[tt], stB[2]], writes=[r_nrmC])
            else:
                transpose_mod(nrmC, r_nrmC, 4 if tt % 2 == 0 else 6, 48, 64, 0,
                              lambda k: act[:, k, tt * 128:(tt + 1) * 128], r_act)
        for step in range(4 + 5):
            for tt in range(4):
                sg = step - tt
                if 0 <= sg <= 5:
                    c_stage(sg, tt)

        def f1_fn(tag, buf, rr):
            jb = tag
            for cc in range(2):
                ch = jb * 2 + cc
                bk = acc6.next()
                for k in range(16):
                    P.add("tensor", lambda e, k=k, bk=bk, cc=cc: e.matmul(
                        bank32(bk), buf[:, k, cc * 128:(cc + 1) * 128], act[:, k, :], start=(k == 0), stop=(k == 15)),
                        reads=rr + [r_act], writes=[r_bank[bk]])
                rl, rrl = rl_p.next()
                P.add("vector", lambda e, bk=bk, ch=ch, rl=rl: e.tensor_scalar(
                    out=rl[:], in0=bank32(bk), scalar1=b1s[:, ch:ch + 1], scalar2=0.0, op0=ALU.add, op1=ALU.max),
                    reads=[r_bank[bk], r_cst], writes=[rrl])
                P.add("scalar", lambda e, ch=ch, rl=rl: e.activation(out=hid[:, ch, :], in_=rl[:], func=AF.Square),
                      reads=[rrl], writes=[r_hid], loose=True)
        stream([(jb, (w1_v, 0, jb * 256)) for jb in range(32)], f1_fn)
        if tb + 1 < 4:
            load_act(tb + 1)

        def f2_fn(tag, buf, rr):
            cb, jg = tag
            for tt in range(4):
                bk = tt
                if jg == 0:
                    P.add("tensor", lambda e, bk=bk: e.matmul(bank32(bk)[:, 0:256], ones_bf[:], brow[1][:, cb * 256:(cb + 1) * 256],
                                                              start=True, stop=False), reads=[r_ln, r_cst], writes=[r_bank[bk]])
                for j in range(16):
                    P.add("tensor", lambda e, j=j, bk=bk, tt=tt: e.matmul(
                        bank32(bk)[:, 0:256], hid[:, jg * 16 + j, tt * 128:(tt + 1) * 128], buf[:, j, :],
                        start=False, stop=(jg == 3 and j == 15)), reads=rr + [r_hid], writes=[r_bank[bk]])
                if jg == 3:
                    tmp, rtmp = tmp_p.next()
                    P.add("vector", lambda e, bk=bk, tmp=tmp: e.tensor_tensor(
                        out=tmp[:], in0=bank32(bk)[:, 0:256], in1=g_bc[1][:, cb * 256:(cb + 1) * 256], op=ALU.mult),
                        reads=[r_bank[bk], r_g[1]], writes=[rtmp])
                    P.add("vector", lambda e, tt=tt, tmp=tmp: e.scalar_tensor_tensor(
                        out=x1[:, tt, cb * 256:(cb + 1) * 256], in0=x1[:, tt, cb * 256:(cb + 1) * 256], scalar=ALPHA,
                        in1=tmp[:], op0=ALU.mult, op1=ALU.add), reads=[rtmp], writes=[r_x1[tt]])
        stream([((cb, jg), (w2_v, jg * 16, cb * 256)) for cb in range(8) for jg in range(4)], f2_fn)

        if tb + 1 < 4:
            outproj(tb)
        else:
            for tt in range(4):
                final_ln(tb, tt)
    return _finish(nc, P, out, A)


def _finish(nc, P, out, A):
    P.emit()
    return nc, P


def _rope_tables(pos):
    row = (pos // 64).astype(np.float32)
    col = (pos % 64).astype(np.float32)
    inv = (10000.0 ** (-np.arange(0, 32, 2, dtype=np.float32) / 32.0)).astype(np.float32)
    ar = row[:, None] * inv[None, :]
    ac = col[:, None] * inv[None, :]
    ang = np.concatenate([ar, ar, ac, ac], axis=-1)
    cos = np.cos(ang).astype(np.float32)
    sin = np.sin(ang).astype(np.float32)
    sgn = np.where((np.arange(64) % 32) < 16, -1.0, 1.0).astype(np.float32)
    sin = sin * sgn[None, :]
    cosT = np.concatenate([cos.T, cos.T], axis=0)
    sinT = np.concatenate([sin.T, sin.T], axis=0)
    return np.ascontiguousarray(cosT), np.ascontiguousarray(sinT)


def _consts():
    c = np.zeros((128, 384), np.float32)
    c[:, 0:128] = np.eye(128, dtype=np.float32)
    pm = np.zeros((128, 128), np.float32)
    for m in range(128):
        d = m % 64
        pd = d + 16 if (d % 32) < 16 else d - 16
        pm[(m // 64) * 64 + pd, m] = 1.0
    c[:, 128:256] = pm
    c[:, 256:384] = 1.0
    return c


def make_in_maps(inp):
    f = lambda a: np.ascontiguousarray(np.asarray(a, dtype=np.float32))
    x = f(inp["x"]); c = f(inp["c"]); ctx = f(inp["ctx"]); c_ctx = f(inp["c_ctx"])
    w_ada = f(inp["w_ada"][0]); b_ada = f(inp["b_ada"][0]); w_in = f(inp["w_in"][0])
    b_glu = f(inp["b_glu"][0])
    shared = {
        "w_ada": w_ada, "w_in": w_in, "w_out": f(inp["w_out"][0]), "w_ff1": f(inp["w_ff1"][0]),
        "w_ff2": f(inp["w_ff2"][0]),
        "b_adaT": f(b_ada.reshape(96, 128).T),
        "b_ada_g": f(np.stack([b_ada[2 * D:3 * D], b_ada[5 * D:6 * D]])),
        "b_gluT": f(b_glu.reshape(16, 128).T),
        "lam4": f(np.stack([inp["lambda_q1"][0], inp["lambda_k1"][0], inp["lambda_q2"][0], inp["lambda_k2"][0]])),
        "subg": f(inp["subln_g"][0].reshape(1, 128)),
        "w_dwT": f(np.asarray(inp["w_dw"][0]).reshape(31, 8, 128).transpose(2, 1, 0).reshape(128, 8 * 31)),
        "cvec": f(np.concatenate([np.asarray(inp[k][0]).reshape(8, 128).T for k in ("b_dw", "conv_ln_g", "conv_ln_b")], axis=1)),
        "b_out": f(inp["b_out"][0].reshape(1, D)),
        "ln1": f(np.stack([inp["ln1_g"][0], inp["ln1_b"][0]])),
        "b1T": f(np.asarray(inp["b_ff1"][0]).reshape(64, 128).T),
        "b_ff2": f(inp["b_ff2"][0].reshape(1, D)),
        "ln2": f(np.stack([inp["ln2_g"][0], inp["ln2_b"][0]])),
        "consts": _consts(),
    }
    maps = []
    for core in range(8):
        b, s = core // 2, core % 2
        own = np.arange(s * T, (s + 1) * T)
        oth = np.arange((1 - s) * T, (2 - s) * T)
        co, so = _rope_tables(own)
        ct, st_ = _rope_tables(oth)
        hm = np.zeros((128, 2), np.float32)
        hm[:, 0] = 1.0 if s == 1 else 0.0
        hm[:, 1] = 1.0 if s == 0 else 0.0
        m = dict(shared)
        m.update({
            "x_own": f(x[b, own]), "x_oth": f(x[b, oth]), "ctxb": f(ctx[b]),
            "cT2": f(np.stack([c[b].reshape(16, 128).T, c_ctx.reshape(16, 128).T], axis=-1).reshape(128, 32)),
            "rope": f(np.stack([co, so, ct, st_])),
            "hmask": hm,
        })
        maps.append(m)
    return maps


_CACHE = {}


def kernel(**inputs):
    if "nc" not in _CACHE:
        _CACHE["nc"] = build_program()[0]
    nc = _CACHE["nc"]
    maps = make_in_maps(inputs)
    res = run_bass_kernel_spmd(nc, maps, core_ids=list(range(8)))
    outp = np.empty((4, L, D), np.float32)
    for core in range(8):
        b, s = core // 2, core % 2
        outp[b, s * T:(s + 1) * T] = np.asarray(res.results[core]["out"], dtype=np.float32)
    return outp
```
